# Optimizing a Trainium2 kernel written in Bass

```python
import jax
import jax.numpy as jnp
from jax import lax
import numpy as np

D_MODEL = 2048
BATCH = 4
SEQ = 4096
DEPTH = 2

GRID_W = 64
CTX_LEN = 256
HEAD_DIM = 128
N_MIX_HEADS = D_MODEL // HEAD_DIM
NA_HEADS = N_MIX_HEADS // 4
GQA_Q_HEADS = N_MIX_HEADS // 4
GQA_KV_HEADS = GQA_Q_HEADS // 2
ML_HEADS = N_MIX_HEADS // 2
NA_W = NA_HEADS * HEAD_DIM
GQA_W = GQA_Q_HEADS * HEAD_DIM
KV_W = GQA_KV_HEADS * HEAD_DIM
ML_W = ML_HEADS * HEAD_DIM
NA_WIN_R = 8
NA_WIN_C = 16
ATTN_BLOCK = 128
ML_CHUNK = 128
ML_CONV = 3
ROPE_THETA = 10000.0
D_FF = 256 * ((8 * D_MODEL // 3 + 255) // 256)
N_MOD = 9
EPS = 1e-6
IN_SPLITS = (NA_W, NA_W, NA_W, GQA_W, KV_W, KV_W, 2 * ML_W, ML_W, ML_W, 4 * ML_HEADS)
IN_W = 3 * NA_W + GQA_W + 2 * KV_W + 4 * ML_W + 4 * ML_HEADS

kernel_name = "hybrid_natten_gqa_mlstm_macaron_dit"


def rms_norm(x, gain):
    xf = x.astype(jnp.float32)
    y = xf * lax.rsqrt(jnp.mean(xf * xf, axis=-1, keepdims=True) + EPS)
    return (y * gain.astype(jnp.float32)).astype(x.dtype)


def modulate(h, shift, scale):
    return h * (1.0 + scale) + shift


def swiglu(h, w_gate, w_up, w_down):
    return (jax.nn.silu(h @ w_gate) * (h @ w_up)) @ w_down


def macaron_ffn(x, gain, mod, w_gate, w_up, w_down):
    shift, scale, gate = mod
    h = modulate(rms_norm(x, gain), shift, scale)
    return x + 0.5 * gate * swiglu(h, w_gate, w_up, w_down)


def to_heads(t, n_heads):
    b, s, _ = t.shape
    return t.reshape(b, s, n_heads, -1).transpose(0, 2, 1, 3)


def from_heads(t):
    b, h, s, d = t.shape
    return t.transpose(0, 2, 1, 3).reshape(b, s, h * d)


def split_columns(p):
    parts, start = [], 0
    for width in IN_SPLITS:
        parts.append(p[..., start:start + width])
        start += width
    return parts


def axial_rope_tables(n_tokens, dim):
    t = jnp.arange(n_tokens)
    row = (t // GRID_W).astype(jnp.float32)
    col = (t % GRID_W).astype(jnp.float32)
    half = dim // 2
    inv_freq = 1.0 / (ROPE_THETA ** (jnp.arange(0, half, 2, dtype=jnp.float32) / half))
    ang_r = row[:, None] * inv_freq[None, :]
    ang_c = col[:, None] * inv_freq[None, :]
    ang = jnp.concatenate([ang_r, ang_r, ang_c, ang_c], axis=-1)
    return jnp.cos(ang), jnp.sin(ang)


def apply_axial_rope(x, cos, sin):
    half = x.shape[-1] // 2

    def rotate_half(u):
        u1, u2 = jnp.split(u, 2, axis=-1)
        return jnp.concatenate([-u2, u1], axis=-1)

    rot = jnp.concatenate([rotate_half(x[..., :half]), rotate_half(x[..., half:])], axis=-1)
    return (x * cos + rot * sin).astype(x.dtype)


def grouped_attention(q, k, v):
    s = jnp.einsum('bhgqd,bhkd->bhgqk', q, k).astype(jnp.float32) * (q.shape[-1] ** -0.5)
    p = jax.nn.softmax(s, axis=-1).astype(v.dtype)
    return jnp.einsum('bhgqk,bhkd->bhgqd', p, v)


def blocked_attention(q, k, v):
    b, hkv, g, t, d = q.shape
    nb = t // ATTN_BLOCK
    qb = q.reshape(b, hkv, g, nb, ATTN_BLOCK, d).transpose(3, 0, 1, 2, 4, 5)
    out = lax.map(lambda qi: grouped_attention(qi, k, v), qb)
    return out.transpose(1, 2, 3, 0, 4, 5).reshape(b, hkv, g, t, d)


def neighbourhood_attention(q, k, v, k_ctx, v_ctx, rpb):
    b, h, t, d = q.shape
    rows = t // GRID_W
    win_r = min(NA_WIN_R, rows)
    r = jnp.arange(rows)
    row_start = jnp.clip(r - win_r // 2, 0, rows - win_r)
    key_rows = row_start[:, None] + jnp.arange(win_r)[None, :]
    cq = jnp.arange(GRID_W)
    col_start = jnp.clip(cq - NA_WIN_C // 2, 0, GRID_W - NA_WIN_C)
    col_in = (cq[None, :] >= col_start[:, None]) & (cq[None, :] < col_start[:, None] + NA_WIN_C)
    qg = q.reshape(b, h, rows, GRID_W, d)
    kg = k.reshape(b, h, rows, GRID_W, d)[:, :, key_rows]
    vg = v.reshape(b, h, rows, GRID_W, d)[:, :, key_rows]
    scale = d ** -0.5
    s_loc = jnp.einsum('bhrqd,bhrikd->bhrqik', qg, kg).astype(jnp.float32) * scale
    dr = key_rows - r[:, None] + (NA_WIN_R - 1)
    dc = jnp.clip(cq[None, :] - cq[:, None], -(NA_WIN_C - 1), NA_WIN_C - 1) + (NA_WIN_C - 1)
    bias = rpb[:, dr[:, None, :, None], dc[None, :, None, :]]
    s_loc = jnp.where(col_in[:, None, :], s_loc + bias.astype(jnp.float32), -jnp.inf)
    s_ctx = jnp.einsum('bhrqd,bhcd->bhrqc', qg, k_ctx).astype(jnp.float32) * scale
    n_loc = win_r * GRID_W
    logits = jnp.concatenate([s_loc.reshape(b, h, rows, GRID_W, n_loc), s_ctx], axis=-1)
    p = jax.nn.softmax(logits, axis=-1).astype(v.dtype)
    p_loc = p[..., :n_loc].reshape(b, h, rows, GRID_W, win_r, GRID_W)
    out = (jnp.einsum('bhrqik,bhrikd->bhrqd', p_loc, vg)
           + jnp.einsum('bhrqc,bhcd->bhrqd', p[..., n_loc:], v_ctx))
    return out.reshape(b, h, t, d)


def centred_depthwise_conv(x, w, bias):
    n_ch = x.shape[-1]
    kw = w.shape[0]
    y = lax.conv_general_dilated(x, w[:, None, :].astype(x.dtype), window_strides=(1,),
                                 padding=[(kw // 2, kw // 2)],
                                 dimension_numbers=('NWC', 'WIO', 'NWC'),
                                 feature_group_count=n_ch)
    return y + bias


def mlstm_streams(qk, v, gates, conv_w, conv_b, gate_b):
    f32 = jnp.float32
    qk = jax.nn.silu(centred_depthwise_conv(qk, conv_w, conv_b))
    q, k = jnp.split(qk, 2, axis=-1)
    q = to_heads(q, ML_HEADS).astype(f32)
    k = to_heads(k, ML_HEADS).astype(f32) * (HEAD_DIM ** -0.5)
    v = to_heads(v, ML_HEADS).astype(f32)
    b, t, _ = gates.shape
    g = (gates + gate_b).astype(f32).reshape(b, t, 4, ML_HEADS).transpose(2, 0, 3, 1)
    log_gates = (g[0], jax.nn.log_sigmoid(g[1]), g[2], jax.nn.log_sigmoid(g[3]))
    return q, k, v, log_gates


def mlstm_chunk_states(k, v, b_cum, log_i, state0):
    b_tot = b_cum[..., -1]
    a = b_tot[..., None] - b_cum + log_i
    a_max = jnp.max(a, axis=-1)
    w = jnp.exp(a - a_max[..., None])
    c_loc = jnp.einsum('bhnl,bhnlk,bhnlv->bhnkv', w, k, v)
    n_loc = jnp.einsum('bhnl,bhnlk->bhnk', w, k)

    def step(carry, xs):
        c_prev, n_prev, m_prev = carry
        c_l, n_l, bt, am = xs
        m_new = jnp.maximum(bt + m_prev, am)
        dec = jnp.exp(bt + m_prev - m_new)
        inp = jnp.exp(am - m_new)
        c_new = dec[..., None, None] * c_prev + inp[..., None, None] * c_l
        n_new = dec[..., None] * n_prev + inp[..., None] * n_l
        return (c_new, n_new, m_new), (c_prev, n_prev, m_prev)

    xs = tuple(jnp.moveaxis(t, 2, 0) for t in (c_loc, n_loc, b_tot, a_max))
    final, prev = lax.scan(step, state0, xs)
    prev = tuple(jnp.moveaxis(t, 0, 2) for t in prev)
    return prev, final


def mlstm_chunk_outputs(q, k, v, b_cum, log_i, prev):
    c_prev, n_prev, m_prev = prev
    length = q.shape[-2]
    seen = jnp.tril(jnp.ones((length, length), dtype=bool))
    dmat = b_cum[..., :, None] - b_cum[..., None, :] + log_i[..., None, :]
    dmat = jnp.where(seen, dmat, -jnp.inf)
    inter = b_cum + m_prev[..., None]
    m = jnp.maximum(inter, jnp.max(dmat, axis=-1))
    s = jnp.einsum('bhnqd,bhnsd->bhnqs', q, k) * jnp.exp(dmat - m[..., None])
    g = jnp.exp(inter - m)
    num = g[..., None] * jnp.einsum('bhnqk,bhnkv->bhnqv', q, c_prev) + jnp.einsum('bhnqs,bhnsv->bhnqv', s, v)
    den = g * jnp.einsum('bhnqk,bhnk->bhnq', q, n_prev) + jnp.sum(s, axis=-1)
    return num / jnp.maximum(jnp.abs(den), jnp.exp(-m))[..., None]


def mlstm_scan(q, k, v, log_i, log_f, state0, with_output):
    b, h, t, d = k.shape
    n = t // ML_CHUNK
    chunk = lambda a: a.reshape(b, h, n, ML_CHUNK, *a.shape[3:])
    qc, kc, vc, lic, lfc = chunk(q), chunk(k), chunk(v), chunk(log_i), chunk(log_f)
    b_cum = jnp.cumsum(lfc, axis=-1)
    prev, final = mlstm_chunk_states(kc, vc, b_cum, lic, state0)
    if not with_output:
        return None, final
    hout = mlstm_chunk_outputs(qc, kc, vc, b_cum, lic, prev).reshape(b, h, t, d)
    return hout, final


def mlstm_bidirectional(lat, ctx, need_ctx):
    ql, kl, vl, gl = lat
    qc, kc, vc, gc = ctx
    b, h, _, d = kl.shape
    f32 = jnp.float32
    zero = (jnp.zeros((b, h, d, d), f32), jnp.zeros((b, h, d), f32), jnp.zeros((b, h), f32))
    rev = lambda a: jnp.flip(a, axis=2)
    hc_f, st_f = mlstm_scan(qc, kc, vc, gc[0], gc[1], zero, need_ctx)
    hl_f, _ = mlstm_scan(ql, kl, vl, gl[0], gl[1], st_f, True)
    hc_b, st_b = mlstm_scan(rev(qc), rev(kc), rev(vc), rev(gc[2]), rev(gc[3]), zero, need_ctx)
    hl_b, _ = mlstm_scan(rev(ql), rev(kl), rev(vl), rev(gl[2]), rev(gl[3]), st_b, True)
    h_lat = hl_f + rev(hl_b)
    h_ctx = hc_f + rev(hc_b) if need_ctx else None
    return h_lat, h_ctx


def mlstm_output(hsum, o_pre, out_norm, dtype):
    hn = rms_norm(hsum, out_norm.reshape(ML_HEADS, 1, HEAD_DIM))
    return (from_heads(hn) * jax.nn.sigmoid(o_pre.astype(jnp.float32))).astype(dtype)


def token_mixer(hl, hc, w_in, rpb, q_norm, k_norm, conv_w, conv_b, gate_b, out_norm, need_ctx):
    b, t, _ = hl.shape
    na_q, na_k, na_v, gq_q, gq_k, gq_v, ml_qk, ml_v, ml_o, ml_g = split_columns(hl @ w_in)
    cna_q, cna_k, cna_v, cgq_q, cgq_k, cgq_v, cml_qk, cml_v, cml_o, cml_g = split_columns(hc @ w_in)

    kc_na = to_heads(cna_k, NA_HEADS)
    vc_na = to_heads(cna_v, NA_HEADS)
    a_lat = neighbourhood_attention(to_heads(na_q, NA_HEADS), to_heads(na_k, NA_HEADS),
                                    to_heads(na_v, NA_HEADS), kc_na, vc_na, rpb)

    grp = GQA_Q_HEADS // GQA_KV_HEADS
    cos, sin = axial_rope_tables(t, HEAD_DIM)
    q_b = apply_axial_rope(rms_norm(to_heads(gq_q, GQA_Q_HEADS), q_norm), cos, sin)
    k_b = apply_axial_rope(rms_norm(to_heads(gq_k, GQA_KV_HEADS), k_norm), cos, sin)
    kc_b = rms_norm(to_heads(cgq_k, GQA_KV_HEADS), k_norm)
    vc_b = to_heads(cgq_v, GQA_KV_HEADS)
    k_all = jnp.concatenate([kc_b, k_b], axis=2)
    v_all = jnp.concatenate([vc_b, to_heads(gq_v, GQA_KV_HEADS)], axis=2)
    b_lat = blocked_attention(q_b.reshape(b, GQA_KV_HEADS, grp, t, HEAD_DIM), k_all, v_all)
    b_lat = b_lat.reshape(b, GQA_Q_HEADS, t, HEAD_DIM)

    lat_c = mlstm_streams(ml_qk, ml_v, ml_g, conv_w, conv_b, gate_b)
    ctx_c = mlstm_streams(cml_qk, cml_v, cml_g, conv_w, conv_b, gate_b)
    h_lat, h_ctx = mlstm_bidirectional(lat_c, ctx_c, need_ctx)
    c_lat = mlstm_output(h_lat, ml_o, out_norm, hl.dtype)

    y_lat = jnp.concatenate([from_heads(a_lat), from_heads(b_lat), c_lat], axis=-1)
    if not need_ctx:
        return y_lat, None
    tc = hc.shape[1]
    a_ctx = grouped_attention(to_heads(cna_q, NA_HEADS)[:, :, None], kc_na, vc_na)[:, :, 0]
    q_bc = rms_norm(to_heads(cgq_q, GQA_Q_HEADS), q_norm).reshape(b, GQA_KV_HEADS, grp, tc, HEAD_DIM)
    b_ctx = grouped_attention(q_bc, kc_b, vc_b).reshape(b, GQA_Q_HEADS, tc, HEAD_DIM)
    c_ctx_out = mlstm_output(h_ctx, cml_o, out_norm, hc.dtype)
    y_ctx = jnp.concatenate([from_heads(a_ctx), from_heads(b_ctx), c_ctx_out], axis=-1)
    return y_lat, y_ctx


def setup_inputs(seed: int = 0) -> dict:
    key = jax.random.key(seed)
    ks = iter(jax.random.split(key, 32))
    f32 = jnp.float32
    D = D_MODEL

    def normal(shape, scale):
        return jax.random.normal(next(ks), shape, f32) * scale

    def gain(shape):
        return 1.0 + normal(shape, 0.05)

    lin = jnp.linspace(3.0, 6.0, ML_HEADS, dtype=f32)
    zer = jnp.zeros((ML_HEADS,), f32)
    gate_base = jnp.concatenate([zer, lin, zer, lin])
    return {
        "x": normal((BATCH, SEQ, D), 1.0),
        "c": normal((BATCH, D), 1.0),
        "ctx": normal((BATCH, CTX_LEN, D), 1.0),
        "c_ctx": normal((D,), 1.0),
        "w_ada": normal((DEPTH, D, N_MOD * D), 0.5 * D ** -0.5),
        "b_ada": normal((DEPTH, N_MOD * D), 0.02),
        "norm_ff1": gain((DEPTH, D)),
        "ff1_gate": normal((DEPTH, D, D_FF), D ** -0.5),
        "ff1_up": normal((DEPTH, D, D_FF), D ** -0.5),
        "ff1_down": normal((DEPTH, D_FF, D), D_FF ** -0.5),
        "norm_mix": gain((DEPTH, D)),
        "w_in": normal((DEPTH, D, IN_W), D ** -0.5),
        "na_rpb": normal((DEPTH, NA_HEADS, 2 * NA_WIN_R - 1, 2 * NA_WIN_C - 1), 0.5),
        "gqa_q_norm": gain((DEPTH, HEAD_DIM)),
        "gqa_k_norm": gain((DEPTH, HEAD_DIM)),
        "ml_conv_w": normal((DEPTH, ML_CONV, 2 * ML_W), ML_CONV ** -0.5),
        "ml_conv_b": normal((DEPTH, 2 * ML_W), 0.02),
        "ml_gate_b": gate_base + normal((DEPTH, 4 * ML_HEADS), 0.1),
        "ml_out_norm": gain((DEPTH, ML_W)),
        "w_out": normal((DEPTH, D, D), D ** -0.5),
        "norm_ff2": gain((DEPTH, D)),
        "ff2_gate": normal((DEPTH, D, D_FF), D ** -0.5),
        "ff2_up": normal((DEPTH, D, D_FF), D ** -0.5),
        "ff2_down": normal((DEPTH, D_FF, D), D_FF ** -0.5),
        "final_norm": gain((D,)),
    }


def reference(x, c, ctx, c_ctx, w_ada, b_ada, norm_ff1, ff1_gate, ff1_up, ff1_down, norm_mix,
              w_in, na_rpb, gqa_q_norm, gqa_k_norm, ml_conv_w, ml_conv_b, ml_gate_b, ml_out_norm,
              w_out, norm_ff2, ff2_gate, ff2_up, ff2_down, final_norm):
    xl, xc = x, ctx
    for l in range(DEPTH):
        last = l == DEPTH - 1
        mod_l = jnp.split((jax.nn.silu(c) @ w_ada[l] + b_ada[l])[:, None, :], N_MOD, axis=-1)
        mod_c = jnp.split(jax.nn.silu(c_ctx) @ w_ada[l] + b_ada[l], N_MOD, axis=-1)
        xl = macaron_ffn(xl, norm_ff1[l], mod_l[0:3], ff1_gate[l], ff1_up[l], ff1_down[l])
        xc = macaron_ffn(xc, norm_ff1[l], mod_c[0:3], ff1_gate[l], ff1_up[l], ff1_down[l])
        hl = modulate(rms_norm(xl, norm_mix[l]), mod_l[3], mod_l[4])
        hc = modulate(rms_norm(xc, norm_mix[l]), mod_c[3], mod_c[4])
        yl, yc = token_mixer(hl, hc, w_in[l], na_rpb[l], gqa_q_norm[l], gqa_k_norm[l],
                             ml_conv_w[l], ml_conv_b[l], ml_gate_b[l], ml_out_norm[l], not last)
        xl = xl + mod_l[5] * (yl @ w_out[l])
        xl = macaron_ffn(xl, norm_ff2[l], mod_l[6:9], ff2_gate[l], ff2_up[l], ff2_down[l])
        if not last:
            xc = xc + mod_c[5] * (yc @ w_out[l])
            xc = macaron_ffn(xc, norm_ff2[l], mod_c[6:9], ff2_gate[l], ff2_up[l], ff2_down[l])
    return rms_norm(xl, final_norm)
```

```python
from contextlib import ExitStack
import numpy as np
import ml_dtypes
import concourse.bass as bass
import concourse.mybir as mybir
from concourse.bass_utils import run_bass_kernel_spmd

F32 = mybir.dt.float32
BF16 = mybir.dt.bfloat16
AF = mybir.ActivationFunctionType
ALU = mybir.AluOpType

D = 2048; T = 4096; C = 256; NTOK = T + C; DFF = 5632; INW = 6688; KC = 16
NCH = NTOK // 128
EPS = 1e-6
NEG = -30000.0
SC = 128 ** -0.5
NCORES = 8
TL = 2048
_DBG_TILES = None

V_CC = 0
V_BADA = V_CC + 32
V_GAIN = V_BADA + 288
V_FIN = V_GAIN + 96
V_CW = V_FIN + 16
V_CB = V_CW + 96
V_QN = V_CB + 32
V_KN = V_QN + 2
V_ON = V_KN + 2
V_GBI = V_ON + 16
V_GBF = V_GBI + 2
V_M = V_GBF + 2
NV = V_M + 2
CF_ID = 0
CF_ROT = CF_ID + 128
CF_MSK = CF_ROT + 128
CF_SEL = CF_MSK + 256
NCF = CF_SEL + 16 * 128
CB_ONE = 0
CB_AVG = 128
CB_AV2 = 256
CB_ID = 384
NCB = 512


class Buf:
    __slots__ = ("w", "r")

    def __init__(self):
        self.w = None
        self.r = {}


class Sched:
    def __init__(self, nc, ndma=48):
        self.nc = nc
        self.eng = {"pe": nc.tensor, "act": nc.scalar, "dve": nc.vector, "pool": nc.gpsimd, "sp": nc.sync}
        self.sems = []
        self.esem = {}
        self.ecnt = {}
        for e in ("pe", "act", "dve", "pool"):
            self.esem[e] = len(self.sems)
            self.sems.append(nc.alloc_semaphore("s_" + e))
            self.ecnt[e] = 0
        self.dsem = []
        self.duse = []
        for i in range(ndma):
            self.dsem.append(len(self.sems))
            self.sems.append(nc.alloc_semaphore("d%d" % i))
            self.duse.append(0)
        self.dk = 0
        self.seen = {e: {} for e in self.eng}
        self.nb = 0
        self.banks = [Buf() for _ in range(8)]
        self.gc = {}

    def bank(self):
        b = self.nb % 8
        self.nb += 1
        return b, self.banks[b]

    def bankg(self, key, banks):
        c = self.gc.get(key, 0)
        self.gc[key] = c + 1
        b = banks[c % len(banks)]
        return b, self.banks[b]

    def _wait(self, e, toks):
        need = {}
        own = self.esem.get(e, -1) if e == "pe" else -1
        for t in toks:
            if t is None:
                continue
            si, v = t
            if si == own:
                continue
            if self.seen[e].get(si, 0) >= v:
                continue
            if need.get(si, 0) < v:
                need[si] = v
        for si, v in need.items():
            self.eng[e].wait_ge(self.sems[si], v)
            self.seen[e][si] = v

    @staticmethod
    def _toks(reads, writes):
        toks = [b.w for b in reads]
        for b in writes:
            toks.append(b.w)
            toks.extend(b.r.items())
        return toks

    @staticmethod
    def _upd(tok, reads, writes):
        for b in writes:
            b.w = tok
            b.r = {}
        for b in reads:
            if b.r.get(tok[0], 0) < tok[1]:
                b.r[tok[0]] = tok[1]

    def op(self, e, fn, reads=(), writes=(), sig=True):
        self._wait(e, self._toks(reads, writes))
        ins = fn(self.eng[e])
        si = self.esem[e]
        if sig:
            ins.then_inc(self.sems[si], 1)
            self.ecnt[e] += 1
            tok = (si, self.ecnt[e])
        else:
            tok = (si, self.ecnt[e] + 1)
        self._upd(tok, reads, writes)

    def dma(self, q, out, in_, reads=(), writes=()):
        toks = self._toks(reads, writes)
        s = self.dk % len(self.dsem)
        self.dk += 1
        si = self.dsem[s]
        if self.duse[s] > 0:
            toks.append((si, 16 * self.duse[s]))
        self._wait(q, toks)
        self.eng[q].dma_start(out=out, in_=in_).then_inc(self.sems[si], 16)
        self.duse[s] += 1
        self._upd((si, 16 * self.duse[s]), reads, writes)

    def barrier(self):
        toks = [(self.esem[e], self.ecnt[e]) for e in self.esem]
        toks += [(self.dsem[s], 16 * self.duse[s]) for s in range(len(self.dsem)) if self.duse[s]]
        for e in self.eng:
            self._wait(e, toks)

    def mm(self, out, lhsT, rhs, start, stop, reads, writes, sig=True):
        self.op("pe", lambda e: e.matmul(out, lhsT, rhs, start=start, stop=stop), reads, writes, sig)

    def act(self, out, in_, func, reads, writes, bias=0.0, scale=1.0):
        self.op("act", lambda e: e.activation(out=out, in_=in_, func=func, bias=bias, scale=scale), reads, writes)

    def tt(self, e, out, in0, in1, op, reads, writes):
        self.op(e, lambda g: g.tensor_tensor(out, in0, in1, op), reads, writes)

    def ts(self, e, out, in0, s1, s2, op0, op1, reads, writes):
        if s2 is None:
            self.op(e, lambda g: g.tensor_scalar(out, in0, s1, None, op0), reads, writes)
        else:
            self.op(e, lambda g: g.tensor_scalar(out, in0, s1, s2, op0, op1), reads, writes)

    def stt(self, out, in0, scalar, in1, op0, op1, reads, writes):
        self.op("dve", lambda g: g.scalar_tensor_tensor(out, in0, scalar, in1, op0, op1), reads, writes)

    def cp(self, e, out, in_, reads, writes):
        if e == "act":
            self.op("act", lambda g: g.copy(out, in_), reads, writes)
        else:
            self.op(e, lambda g: g.tensor_copy(out, in_), reads, writes)


class Stream:
    def __init__(self, slots, loads):
        self.slots = slots
        self.loads = loads
        self.n = 0

    def ensure(self, i):
        i = min(i, len(self.loads) - 1)
        while self.n <= i:
            self.loads[self.n](self.slots[self.n % len(self.slots)])
            self.n += 1

    def get(self, i):
        self.ensure(i + len(self.slots) - 1)
        return self.slots[i % len(self.slots)]


def build(stop=None, nlayers=2):
    nc = bass.Bass("TRN2", target_bir_lowering=False)
    S = Sched(nc)
    din = lambda n, sh, dt=F32: nc.dram_tensor(n, list(sh), dt, kind="ExternalInput").ap()
    x0 = din("xT0", [D, NTOK])
    vec_d = din("vec", [128, NV])
    cf_d = din("cf", [128, NCF])
    cb_d = din("cb", [128, NCB], BF16)
    rope_d = din("rope", [2, 128, T])
    nabt_d = din("nabt", [2, 4, 128, 16, 64])
    w_ada = din("w_ada", [2, D, 9 * D])
    ffw = {}
    for nm in ("ff1_gate", "ff1_up", "ff2_gate", "ff2_up"):
        ffw[nm] = din(nm, [2, D, DFF])
    for nm in ("ff1_down", "ff2_down"):
        ffw[nm] = din(nm, [2, DFF, D])
    w_in = din("w_in", [2, D, INW])
    w_out = din("w_out", [2, D, D])
    out_d = nc.dram_tensor("outT", [D, TL], F32, kind="ExternalOutput").ap()
    dscr = lambda n, sh, dt: nc.dram_tensor(n, list(sh), dt, kind="Internal").ap()
    XT = dscr("XT", [D, NTOK], F32)
    HT = dscr("HT", [D, NTOK], BF16)
    PF = dscr("PF", [4864, NTOK], BF16)
    QKR = dscr("QKR", [2048, NTOK], BF16)
    VT = dscr("VT", [NTOK, 1792], BF16)
    YT = dscr("YT", [D, NTOK], BF16)
    XO = dscr("XO", [D, TL], F32)
    R_NAQ, R_NAK, R_GQ, R_GK, R_MQ, R_MK, R_MO = 0, 512, 1024, 1536, 1792, 2816, 3840
    bXT, bHT, bPF, bQKR, bVT, bYT = Buf(), Buf(), Buf(), Buf(), Buf(), Buf()
    bXO = Buf()

    ps = nc.alloc_psum_tensor("ps", [128, 8, 512], F32)

    def PS(b, w=512, o=0):
        return ps[:, b, o:o + w]

    TILES = _DBG_TILES or ([(i * 1024, 1024, 0) for i in range(4)] + [(T, C, 1)])

    def blocks(n):
        return [(o, min(512, n - o)) for o in range(0, n, 512)]

    with ExitStack() as top:
        _ctr = [0]

        def sb(es, n, sh, dt):
            _ctr[0] += 1
            return es.enter_context(nc.sbuf_tensor("%s_s%d" % (n, _ctr[0]), list(sh), dt))
        vec = sb(top, "vec", [128, NV], F32)
        cf = sb(top, "cf", [128, NCF], F32)
        cb = sb(top, "cb", [128, NCB], BF16)
        modT = sb(top, "modT", [128, 2, 144, 2], F32)
        AB = sb(top, "AB", [128, 2, 3, 3, 16, 2], F32)
        smallv = sb(top, "smallv", [128, 8], F32)
        bC = Buf()
        S.dma("sp", vec[:], vec_d, writes=[bC])
        S.dma("sp", cf[:], cf_d, writes=[bC])
        S.dma("sp", cb[:], cb_d, writes=[bC])
        ident = cf[:, CF_ID:CF_ID + 128]
        ones_b = cb[:, CB_ONE:CB_ONE + 128]
        avg_b = cb[:, CB_AVG:CB_AVG + 128]
        av2_b = cb[:, CB_AV2:CB_AV2 + 128]
        ident_b = cb[:, CB_ID:CB_ID + 128]

        with ExitStack() as es:
            xb = [sb(es, "x0b%d" % i, [128, 4, 1024], F32) for i in range(2)]
            xbb = [Buf(), Buf()]
            i = 0
            x0v = x0.rearrange("(c p) t -> p c t", p=128)
            XTv = XT.rearrange("(c p) t -> p c t", p=128)
            for (t0, n, s) in TILES:
                for c4 in range(4):
                    S.dma("sp", xb[i % 2][:, :, :n], x0v[:, c4 * 4:c4 * 4 + 4, t0:t0 + n], writes=[xbb[i % 2]])
                    S.dma("sp", XTv[:, c4 * 4:c4 * 4 + 4, t0:t0 + n], xb[i % 2][:, :, :n], reads=[xbb[i % 2]], writes=[bXT])
                    i += 1
            S.barrier()

        with ExitStack() as es:
            scT = sb(es, "scT", [128, 16, 2], BF16)
            wa = [sb(es, "wa%d" % i, [128, 16, 512], BF16) for i in range(3)]
            wab = [Buf() for _ in range(3)]
            S.act(scT[:], vec[:, V_CC:V_CC + 32].rearrange("p (k s) -> p k s", s=2), AF.Silu, [bC], [bC])
            for l in range(nlayers):
                wv = w_ada[l].rearrange("(k p) n -> p k n", p=128)
                loads = [(lambda slot, j=j: S.dma("pool", slot[0][:], wv[:, :, j * 512:(j + 1) * 512], writes=[slot[1]]))
                         for j in range(36)]
                st = Stream(list(zip(wa, wab)), loads)
                for j in range(36):
                    wt, wb = st.get(j)
                    bk, bb = S.bank()
                    for cc in range(4):
                        for k in range(16):
                            S.mm(PS(bk, 2, cc * 2), wt[:, k, cc * 128:(cc + 1) * 128], scT[:, k, :], k == 0, k == 15,
                                 [wb, bC], [bb], sig=(cc == 3 and k == 15))
                    for s in range(2):
                        S.tt("dve", modT[:, l, j * 4:(j + 1) * 4, s], ps[:, bk, s:8:2],
                             vec[:, V_BADA + l * 144 + j * 4:V_BADA + l * 144 + j * 4 + 4], ALU.add, [bb, bC], [bC])
                for sub in range(3):
                    gain = vec[:, V_GAIN + (l * 3 + sub) * 16:V_GAIN + (l * 3 + sub) * 16 + 16]
                    for s in range(2):
                        S.stt(AB[:, l, sub, 0, :, s], modT[:, l, (sub * 3 + 1) * 16:(sub * 3 + 2) * 16, s], 1.0, gain,
                              ALU.add, ALU.mult, [bC], [bC])
                        S.cp("dve", AB[:, l, sub, 1, :, s], modT[:, l, (sub * 3) * 16:(sub * 3 + 1) * 16, s], [bC], [bC])
                        S.ts("dve", AB[:, l, sub, 2, :, s], modT[:, l, (sub * 3 + 2) * 16:(sub * 3 + 3) * 16, s],
                             0.5 if sub != 1 else 1.0, None, ALU.mult, None, [bC], [bC])
            S.barrier()

        if stop == "ada":
            S.dma("sp", out_d[0:128, 0:576], modT[:].rearrange("p a b c -> p (a b c)"), reads=[bC], writes=[Buf()])
            S.barrier()
            return nc
        def norm_block(l, sub, s, xs, bxs, hT, bhT, o, w, sq, bsq, tmp, btmp, gainvec=None):
            S.act(sq[:, :, :w], xs[:, :, o:o + w], AF.Square, [bxs], [bsq])
            bk, bb = S.bank()
            for c in range(16):
                S.mm(PS(bk, w), avg_b, sq[:, c, :w], c == 0, c == 15, [bsq, bC], [bb], sig=(c == 15))
            S.act(tmp[:, 0, :w], PS(bk, w), AF.Sqrt, [bb], [btmp], bias=smallv[:, 0:1])
            S.op("dve", lambda g: g.reciprocal(tmp[:, 0, :w], tmp[:, 0, :w]), [btmp], [btmp])
            for c in range(16):
                if gainvec is None:
                    a_ap = AB[:, l, sub, 0, c, s:s + 1]
                    b_ap = AB[:, l, sub, 1, c, s:s + 1]
                else:
                    a_ap = gainvec[:, c:c + 1]
                    b_ap = None
                S.stt(tmp[:, 1 + c % 2, :w], xs[:, c, o:o + w], a_ap, tmp[:, 0, :w], ALU.mult, ALU.mult, [bxs, btmp, bC], [btmp])
                if b_ap is None:
                    S.cp("act", hT[:, c, o:o + w], tmp[:, 1 + c % 2, :w], [btmp], [bhT])
                else:
                    S.act(hT[:, c, o:o + w], tmp[:, 1 + c % 2, :w], AF.Identity, [btmp, bC], [bhT], bias=b_ap)

        S.op("dve", lambda g: g.memset(smallv[:, 0:1], EPS), [], [bC])
        S.op("dve", lambda g: g.memset(smallv[:, 3:4], 1.0), [], [bC])
        XTv = XT.rearrange("(c p) t -> p c t", p=128)
        XOv = XO.rearrange("(c p) t -> p c t", p=128)
        OWN = [(0, 1024, 0), (1024, 1024, 0)]
        HTv = HT.rearrange("(c p) t -> p c t", p=128)

        def ffn_stage(l, which, tiles=None, xv=None, bxv=None):
            xv = XTv if xv is None else xv
            bxv = bXT if bxv is None else bxv
            sub = 0 if which == 0 else 2
            pre = "ff1" if which == 0 else "ff2"
            wg = ffw[pre + "_gate"][l].rearrange("(k p) n -> p k n", p=128)
            wu = ffw[pre + "_up"][l].rearrange("(k p) n -> p k n", p=128)
            wd = ffw[pre + "_down"][l].rearrange("(f p) n -> p f n", p=128)
            groups = [(g * 8, 8) for g in range(5)] + [(40, 4)]
            tiles = tiles or TILES
            with ExitStack() as es:
                xs = sb(es, "xs", [128, 16, 1024], F32); bxs = Buf()
                hT = sb(es, "hT", [128, 16, 1024], BF16); bhT = Buf()
                ab = sb(es, "ab", [128, 8, 1024], BF16); bab = Buf()
                sq = ab[:].rearrange("p a (b c) -> p (a b) c", b=2)
                tmp = sb(es, "tmp", [128, 3, 512], F32); btmp = Buf()
                gsb = [sb(es, "gsb%d" % i, [128, 512], F32) for i in range(2)]; bgs = [Buf(), Buf()]
                gus = [(sb(es, "wg%d" % i, [128, 16, 256], BF16), sb(es, "wu%d" % i, [128, 16, 256], BF16), Buf()) for i in range(2)]
                wds = [(sb(es, "wd%d" % i, [128, 8, 512], BF16), Buf()) for i in range(2)]
                gu_loads, wd_loads = [], []
                for (t0, n, s) in tiles:
                    for (f0, nf) in groups:
                        for pc in range(nf // 2):
                            c0 = (f0 + pc * 2) * 128

                            def ld(slot, c0=c0):
                                S.dma("pool", slot[0][:], wg[:, :, c0:c0 + 256], writes=[slot[2]])
                                S.dma("pool", slot[1][:], wu[:, :, c0:c0 + 256], writes=[slot[2]])
                            gu_loads.append(ld)
                        for dp in range(4):
                            wd_loads.append(lambda slot, f0=f0, nf=nf, dp=dp: S.dma(
                                "pool", slot[0][:, :nf, :], wd[:, f0:f0 + nf, dp * 512:(dp + 1) * 512], writes=[slot[1]]))
                gst = Stream(gus, gu_loads)
                dst = Stream(wds, wd_loads)
                gi = 0
                di = 0
                gk = 0
                for (t0, n, s) in tiles:
                    blks = blocks(n)
                    for c4 in range(4):
                        S.dma("sp", xs[:, c4 * 4:c4 * 4 + 4, :n], xv[:, c4 * 4:c4 * 4 + 4, t0:t0 + n], reads=[bxv], writes=[bxs])
                    for (o, w) in blks:
                        norm_block(l, sub, s, xs, bxs, hT, bhT, o, w, sq, bab, tmp, btmp)
                    for (f0, nf) in groups:
                        for pc in range(nf // 2):
                            wgt, wut, wb = gst.get(gi); gi += 1
                            for fc in range(2):
                                fi = pc * 2 + fc
                                bks = {}
                                for mi, wt in enumerate((wgt, wut)):
                                    bks[mi] = [S.bank() for _ in blks]
                                    for k in range(16):
                                        for bi, (o, w) in enumerate(blks):
                                            bk, bb = bks[mi][bi]
                                            S.mm(PS(bk, w), wt[:, k, fc * 128:(fc + 1) * 128], hT[:, k, o:o + w], k == 0, k == 15,
                                                 [wb, bhT], [bb], sig=(k == 15 and bi == len(blks) - 1))
                                for bi, (o, w) in enumerate(blks):
                                    g_ = gk % 2; gk += 1
                                    S.act(gsb[g_][:, :w], PS(bks[0][bi][0], w), AF.Silu, [bks[0][bi][1]], [bgs[g_]])
                                    S.tt("dve", ab[:, fi, o:o + w], gsb[g_][:, :w], PS(bks[1][bi][0], w), ALU.mult,
                                         [bgs[g_], bks[1][bi][1]], [bab])
                            if pc == 0:
                                dst.ensure(di + 1)
                        gst.ensure(gi + 1)
                        for dp in range(4):
                            wdt, wdb = dst.get(di); di += 1
                            for dc in range(4):
                                d = dp * 4 + dc
                                bkl = [S.bank() for _ in blks]
                                for fi in range(nf):
                                    for bi, (o, w) in enumerate(blks):
                                        bk, bb = bkl[bi]
                                        S.mm(PS(bk, w), wdt[:, fi, dc * 128:(dc + 1) * 128], ab[:, fi, o:o + w], fi == 0, fi == nf - 1,
                                             [wdb, bab], [bb], sig=(fi == nf - 1 and bi == len(blks) - 1))
                                for bi, (o, w) in enumerate(blks):
                                    bk, bb = bkl[bi]
                                    S.stt(xs[:, d, o:o + w], PS(bk, w), AB[:, l, sub, 2, d, s:s + 1], xs[:, d, o:o + w],
                                          ALU.mult, ALU.add, [bb, bxs, bC], [bxs])
                    for c4 in range(4):
                        S.dma("sp", xv[:, c4 * 4:c4 * 4 + 4, t0:t0 + n], xs[:, c4 * 4:c4 * 4 + 4, :n], reads=[bxs], writes=[bxv])
                S.barrier()

        def mixnorm_stage(l):
            with ExitStack() as es:
                xs = sb(es, "xs", [128, 16, 1024], F32); bxs = Buf()
                hT = sb(es, "hT", [128, 16, 1024], BF16); bhT = Buf()
                sq = sb(es, "sq", [128, 16, 512], BF16); bsq = Buf()
                tmp = sb(es, "tmp", [128, 3, 512], F32); btmp = Buf()
                for (t0, n, s) in TILES:
                    for c4 in range(4):
                        S.dma("sp", xs[:, c4 * 4:c4 * 4 + 4, :n], XTv[:, c4 * 4:c4 * 4 + 4, t0:t0 + n], reads=[bXT], writes=[bxs])
                    for (o, w) in blocks(n):
                        norm_block(l, 1, s, xs, bxs, hT, bhT, o, w, sq, bsq, tmp, btmp)
                    for c4 in range(4):
                        S.dma("sp", HTv[:, c4 * 4:c4 * 4 + 4, t0:t0 + n], hT[:, c4 * 4:c4 * 4 + 4, :n], reads=[bhT], writes=[bHT])
                S.barrier()

        def final_stage():
            outv = out_d.rearrange("(c p) t -> p c t", p=128)
            with ExitStack() as es:
                xs = sb(es, "xs", [128, 16, 1024], F32); bxs = Buf()
                ho = sb(es, "ho", [128, 16, 1024], F32); bho = Buf()
                sq = sb(es, "sq", [128, 16, 512], BF16); bsq = Buf()
                tmp = sb(es, "tmp", [128, 3, 512], F32); btmp = Buf()
                bout = Buf()
                for (t0, n, s) in OWN:
                    for c4 in range(4):
                        S.dma("sp", xs[:, c4 * 4:c4 * 4 + 4, :n], XOv[:, c4 * 4:c4 * 4 + 4, t0:t0 + n], reads=[bXO], writes=[bxs])
                    for (o, w) in blocks(n):
                        norm_block(0, 0, 0, xs, bxs, ho, bho, o, w, sq, bsq, tmp, btmp, gainvec=vec[:, V_FIN:V_FIN + 16])
                    for c4 in range(4):
                        S.dma("sp", outv[:, c4 * 4:c4 * 4 + 4, t0:t0 + n], ho[:, c4 * 4:c4 * 4 + 4, :n], reads=[bho], writes=[bout])
                S.barrier()

        def dump_stage(src, rows):
            with ExitStack() as es:
                xb = sb(es, "dmp", [128, TL], F32); bx = Buf()
                for c in range(rows // 128):
                    S.dma("sp", xb[:], src[c * 128:(c + 1) * 128, 0:TL], writes=[bx])
                    S.dma("sp", out_d[c * 128:(c + 1) * 128, :], xb[:], reads=[bx], writes=[Buf()])
                S.barrier()

        def dump_bf16(src, rows, tok0=0):
            with ExitStack() as es:
                xb = sb(es, "dmpb", [128, TL], BF16); bx = Buf()
                xf = sb(es, "dmpf", [128, TL], F32); bf = Buf()
                for c in range(rows // 128):
                    S.dma("sp", xb[:], src[c * 128:(c + 1) * 128, tok0:tok0 + TL], writes=[bx])
                    S.cp("dve", xf[:], xb[:], [bx], [bf])
                    S.dma("sp", out_d[c * 128:(c + 1) * 128, :], xf[:], reads=[bf], writes=[Buf()])
                S.barrier()

        PIECES = [(0, 4), (4, 4), (12, 4), (16, 2), (20, 4), (24, 4), (28, 4), (32, 4), (44, 4), (48, 4)]

        def chunk_info(ch):
            if ch < 4:
                return "scale", PF, R_NAQ + ch * 128
            if ch < 8:
                return "copy", PF, R_NAK + (ch - 4) * 128
            if ch < 16:
                return "q", PF, R_GQ + (ch - 12) * 128
            if ch < 18:
                return "k", PF, R_GK + (ch - 16) * 128
            if ch < 36:
                return "copy", QKR, (ch - 20) * 128
            return "sig", PF, R_MO + (ch - 44) * 128

        def proj_stage(l, GI, GF, bG):
            wv = w_in[l].rearrange("(k p) n -> p k n", p=128)
            with ExitStack() as es:
                hTs = [(sb(es, "phT%d" % i, [128, 16, 1024], BF16), Buf()) for i in range(2)]
                wps = [(sb(es, "pw%d" % i, [128, 16, 512], BF16), Buf()) for i in range(2)]
                stg = [(sb(es, "pst%d" % i, [128, 1024], BF16), Buf()) for i in range(3)]
                WI = sb(es, "WI", [128, 16, 40], BF16); WF = sb(es, "WF", [128, 16, 40], BF16); bW = Buf()
                sqb = sb(es, "psq", [128, 512], BF16); bsq = Buf()
                f32t = sb(es, "pf32", [128, 5, 512], F32); bft = [Buf() for _ in range(5)]
                cs = [(sb(es, "pcs%d" % i, [128, 2, 512], F32), Buf()) for i in range(2)]
                gtmp = sb(es, "gtmp", [40, 512], F32); bgt = Buf()
                S.op("dve", lambda g: g.memset(WI[:], 0.0), [], [bW])
                S.op("dve", lambda g: g.memset(WF[:], 0.0), [], [bW])
                for (dst, c0, o0) in ((WI, 6656, 0), (WF, 6664, 0), (WI, 6672, 32), (WF, 6680, 32)):
                    S.dma("pool", dst[:, :, o0:o0 + 8], wv[:, :, c0:c0 + 8], writes=[bW])
                S.ts("dve", smallv[:, 1:2], vec[:, V_QN + l:V_QN + l + 1], SC, None, ALU.mult, None, [bC], [bC])
                S.ts("dve", smallv[:, 2:3], vec[:, V_GBF + l:V_GBF + l + 1], -1.0, None, ALU.mult, None, [bC], [bC])
                ht_loads = []
                for (t0, n, s) in TILES:
                    def ldh(slot, t0=t0, n=n):
                        for c4 in range(4):
                            S.dma("sp", slot[0][:, c4 * 4:c4 * 4 + 4, :n], HTv[:, c4 * 4:c4 * 4 + 4, t0:t0 + n], reads=[bHT], writes=[slot[1]])
                    ht_loads.append(ldh)
                hst = Stream(hTs, ht_loads)
                w_loads = []
                for _ in TILES:
                    for (c0, ncn) in PIECES:
                        w_loads.append(lambda slot, c0=c0, ncn=ncn: S.dma(
                            "pool", slot[0][:, :, :ncn * 128], wv[:, :, c0 * 128:(c0 + ncn) * 128], writes=[slot[1]]))
                wst = Stream(wps, w_loads)
                wi = 0
                sk = 0
                for ti, (t0, n, s) in enumerate(TILES):
                    hT, bhT = hst.get(ti)
                    blks = blocks(n)
                    if s == 0:
                        for bi, (o, w) in enumerate(blks):
                            S.dma("sp", cs[bi][0][:, :, :w], rope_d[:, :, t0 + o:t0 + o + w].rearrange("a p t -> p a t"), writes=[cs[bi][1]])
                    for (o, w) in blks:
                        for (Wt, dstG, isf) in ((WI, GI, False), (WF, GF, True)):
                            bk, bb = S.bank()
                            for k in range(16):
                                S.mm(ps[0:40, bk, 0:w], Wt[:, k, :], hT[:, k, o:o + w], k == 0, k == 15, [bW, bhT], [bb], sig=(k == 15))
                            if not isf:
                                S.act(dstG[0:40, t0 + o:t0 + o + w], ps[0:40, bk, 0:w], AF.Identity, [bb, bC], [bG],
                                      bias=vec[0:40, V_GBI + l:V_GBI + l + 1])
                            else:
                                S.act(gtmp[:, :w], ps[0:40, bk, 0:w], AF.Exp, [bb, bC], [bgt], bias=smallv[0:40, 2:3], scale=-1.0)
                                S.act(gtmp[:, :w], gtmp[:, :w], AF.Ln, [bgt], [bgt], bias=smallv[0:40, 3:4])
                                S.ts("dve", dstG[0:40, t0 + o:t0 + o + w], gtmp[:, :w], -1.0, None, ALU.mult, None, [bgt], [bG])
                    for (c0, ncn) in PIECES:
                        wt, wb = wst.get(wi); wi += 1
                        for cc in range(ncn):
                            ch = c0 + cc
                            kind, dst, row = chunk_info(ch)
                            st_t, st_b = stg[sk % 3]; sk += 1
                            bkl = [S.bank() for _ in blks]
                            for k in range(16):
                                for bi, (o, w) in enumerate(blks):
                                    S.mm(PS(bkl[bi][0], w), wt[:, k, cc * 128:(cc + 1) * 128], hT[:, k, o:o + w], k == 0, k == 15,
                                         [wb, bhT], [bkl[bi][1]], sig=(k == 15 and bi == len(blks) - 1))
                            for bi, (o, w) in enumerate(blks):
                                bk, bb = bkl[bi]
                                if kind == "scale":
                                    S.act(st_t[:, o:o + w], PS(bk, w), AF.Identity, [bb], [st_b], scale=SC)
                                elif kind == "copy":
                                    if bi % 2 == 0:
                                        S.cp("dve", st_t[:, o:o + w], PS(bk, w), [bb], [st_b])
                                    else:
                                        S.cp("act", st_t[:, o:o + w], PS(bk, w), [bb], [st_b])
                                elif kind == "sig":
                                    S.act(st_t[:, o:o + w], PS(bk, w), AF.Sigmoid, [bb], [st_b])
                                else:
                                    gcol = smallv[:, 1:2] if kind == "q" else vec[:, V_KN + l:V_KN + l + 1]
                                    S.act(sqb[:, :w], PS(bk, w), AF.Square, [bb], [bsq])
                                    b2, bb2 = S.bank()
                                    S.mm(PS(b2, w), av2_b, sqb[:, :w], True, True, [bsq, bC], [bb2])
                                    S.act(f32t[:, 0, :w], PS(b2, w), AF.Sqrt, [bb2, bC], [bft[0]], bias=smallv[:, 0:1])
                                    S.op("dve", lambda g, w=w: g.reciprocal(f32t[:, 0, :w], f32t[:, 0, :w]), [bft[0]], [bft[0]])
                                    S.stt(f32t[:, 1, :w], PS(bk, w), gcol, f32t[:, 0, :w], ALU.mult, ALU.mult, [bb, bft[0], bC], [bft[1]])
                                    if s == 0:
                                        b3, bb3 = S.bank()
                                        S.mm(PS(b3, w), cf[:, CF_ROT:CF_ROT + 128], f32t[:, 1, :w], True, True, [bft[1], bC], [bb3])
                                        S.tt("dve", f32t[:, 2, :w], f32t[:, 1, :w], cs[bi][0][:, 0, :w], ALU.mult, [bft[1], cs[bi][1]], [bft[2]])
                                        S.tt("dve", f32t[:, 3, :w], PS(b3, w), cs[bi][0][:, 1, :w], ALU.mult, [bb3, cs[bi][1]], [bft[3]])
                                        S.tt("pool", st_t[:, o:o + w], f32t[:, 2, :w], f32t[:, 3, :w], ALU.add, [bft[2], bft[3]], [st_b])
                                    else:
                                        S.cp("act", st_t[:, o:o + w], f32t[:, 1, :w], [bft[1]], [st_b])
                            S.dma("sp", dst[row:row + 128, t0:t0 + n], st_t[:, :n], reads=[st_b], writes=[bPF if dst is PF else bQKR])
                S.barrier()

        def vproj_stage(l):
            wv = w_in[l].rearrange("(k p) n -> p k n", p=128)
            with ExitStack() as es:
                hTs = [(sb(es, "vhT%d" % i, [128, 16, 1024], BF16), Buf()) for i in range(2)]
                WV = sb(es, "WV", [128, 16, 1792], BF16); bWV = Buf()
                stg = [(sb(es, "vst%d" % i, [128, 1792], BF16), Buf()) for i in range(2)]
                for (c0, n_, o0) in ((1024, 512, 0), (2304, 256, 512), (4608, 512, 768), (5120, 512, 1280)):
                    S.dma("pool", WV[:, :, o0:o0 + n_], wv[:, :, c0:c0 + n_], writes=[bWV])
                ht_loads = []
                for (t0, n, s) in TILES:
                    def ldh(slot, t0=t0, n=n):
                        for c4 in range(4):
                            S.dma("sp", slot[0][:, c4 * 4:c4 * 4 + 4, :n], HTv[:, c4 * 4:c4 * 4 + 4, t0:t0 + n], reads=[bHT], writes=[slot[1]])
                    ht_loads.append(ldh)
                hst = Stream(hTs, ht_loads)
                sk = 0
                cbl = [(0, 512), (512, 512), (1024, 512), (1536, 256)]
                for ti, (t0, n, s) in enumerate(TILES):
                    hT, bhT = hst.get(ti)
                    for tt_ in range(n // 128):
                        bkl = [S.bank() for _ in cbl]
                        for k in range(16):
                            for ci, (c0, w) in enumerate(cbl):
                                S.mm(PS(bkl[ci][0], w), hT[:, k, tt_ * 128:(tt_ + 1) * 128], WV[:, k, c0:c0 + w], k == 0, k == 15,
                                     [bWV, bhT], [bkl[ci][1]], sig=(k == 15 and ci == 3))
                        st_t, st_b = stg[sk % 2]; sk += 1
                        for ci, (c0, w) in enumerate(cbl):
                            S.cp("dve" if ci % 2 == 0 else "act", st_t[:, c0:c0 + w], PS(bkl[ci][0], w), [bkl[ci][1]], [st_b])
                        S.dma("sp", VT[t0 + tt_ * 128:t0 + (tt_ + 1) * 128, :], st_t[:], reads=[st_b], writes=[bVT])
                S.barrier()

        VTv = VT.rearrange("(t p) d -> p t d", p=128)

        def gqa_stage(l, need_ctx, own=False):
            with ExitStack() as es:
                kT = sb(es, "gkT", [128, NTOK], BF16); bk_ = Buf()
                Vt = sb(es, "gV", [128, NCH, 128], BF16); bv_ = Buf()
                qT = sb(es, "gqT", [128, NTOK], BF16); bq_ = Buf()
                pT = [(sb(es, "gpT%d" % i, [128, 512], BF16), Buf()) for i in range(4)]
                rd = sb(es, "grd", [128, 512], F32); brd = Buf()
                ys = [(sb(es, "gys%d" % i, [128, 512], BF16), Buf()) for i in range(2)]
                qo = [(sb(es, "gqo%d" % i, [128, 512], BF16), Buf()) for i in range(2)]
                qk_ = 0
                yk = 0
                pk = 0
                for g in range(2):
                    S.dma("sp", kT[:], PF[R_GK + g * 128:R_GK + (g + 1) * 128, :], reads=[bPF], writes=[bk_])
                    S.dma("sp", Vt[:], VTv[:, :, 512 + g * 128:512 + (g + 1) * 128], reads=[bVT], writes=[bv_])
                    for qh in range(2):
                        h = 2 * g + qh
                        S.dma("sp", qT[:], PF[R_GQ + h * 128:R_GQ + (h + 1) * 128, :], reads=[bPF], writes=[bq_])
                        qblocks = [(i * 512, 512, list(range(NCH))) for i in range(4 if own else 8)]
                        if need_ctx:
                            qblocks.append((T, C, [32, 33]))
                        for (q0, w, kts) in qblocks:
                            if own:
                                qs_t, qs_b = qo[qk_ % 2]; qk_ += 1
                                S.act(qs_t[:, :w], qT[:, q0:q0 + w], AF.Identity, [bq_, bC], [qs_b], scale=vec[:, V_M:V_M + 1])
                                S.stt(qs_t[:, :w], qT[:, TL + q0:TL + q0 + w], vec[:, V_M + 1:V_M + 2], qs_t[:, :w], ALU.mult, ALU.add, [bq_, qs_b, bC], [qs_b])
                                q_ap, q_bf = qs_t[:, :w], qs_b
                            else:
                                q_ap, q_bf = qT[:, q0:q0 + w], bq_
                            bo, bbo = S.bankg('o', [0, 1])
                            bd, bbd = S.bankg('d', [2, 3])
                            pend = None
                            for i, kt in enumerate(kts + [None]):
                                cur = None
                                if kt is not None:
                                    bs, bbs = S.bankg('s', [4, 5, 6, 7])
                                    S.mm(PS(bs, w), kT[:, kt * 128:(kt + 1) * 128], q_ap, True, True, [bk_, q_bf], [bbs])
                                    p_t, p_b = pT[pk % 4]; pk += 1
                                    S.act(p_t[:, :w], PS(bs, w), AF.Exp, [bbs], [p_b])
                                    cur = (kt, p_t, p_b, i)
                                if pend is not None:
                                    kt_, p_t_, p_b_, i_ = pend
                                    S.mm(PS(bo, w), Vt[:, kt_, :], p_t_[:, :w], i_ == 0, i_ == len(kts) - 1, [bv_, p_b_], [bbo], sig=False)
                                    S.mm(PS(bd, w), ones_b, p_t_[:, :w], i_ == 0, i_ == len(kts) - 1, [bC, p_b_], [bbd])
                                pend = cur
                            S.op("dve", lambda e, w=w, bd=bd: e.reciprocal(rd[:, :w], PS(bd, w)), [bbd], [brd])
                            y_t, y_b = ys[yk % 2]; yk += 1
                            S.tt("dve", y_t[:, :w], PS(bo, w), rd[:, :w], ALU.mult, [bbo, brd], [y_b])
                            S.dma("sp", YT[512 + h * 128:512 + (h + 1) * 128, q0:q0 + w], y_t[:, :w], reads=[y_b], writes=[bYT])
                            if own:
                                S.dma("sp", YT[512 + h * 128:512 + (h + 1) * 128, TL + q0:TL + q0 + w], y_t[:, :w], reads=[y_b], writes=[bYT])
                S.barrier()

        def na_rows(need_ctx):
            rows = []
            for r in range(64):
                rs = min(max(r - 4, 0), 56)
                tl = []
                for j in range(rs // 2, (rs + 7) // 2 + 1):
                    v0 = rs <= 2 * j <= rs + 7
                    v1 = rs <= 2 * j + 1 <= rs + 7
                    d0 = 2 * j - r
                    if v0 and v1:
                        idx = d0 + 7
                        assert 0 <= idx < 14
                    elif v1:
                        assert d0 == -5
                        idx = 14
                    else:
                        assert v0 and d0 == 3
                        idx = 15
                    tl.append((j, idx))
                tl += [(32, None), (33, None)]
                rows.append((r * 64, tl))
            if need_ctx:
                for i in range(4):
                    rows.append((T + i * 64, [(32, None), (33, None)]))
            return rows

        def na_stage(l, need_ctx):
            rows = na_rows(need_ctx)
            with ExitStack() as es:
                kT = sb(es, "nkT", [128, NTOK], BF16); bk_ = Buf()
                Vt = sb(es, "nV", [128, NCH, 128], BF16); bv_ = Buf()
                qT = sb(es, "nqT", [128, NTOK], BF16); bq_ = Buf()
                BT = sb(es, "nBT", [128, 16, 64], F32); bbt = Buf()
                BTb = sb(es, "nBTb", [128, 16, 64], BF16); bbtb = Buf()
                pT = [(sb(es, "npT%d" % i, [128, 512], BF16), Buf()) for i in range(4)]
                rd = sb(es, "nrd", [128, 512], F32); brd = Buf()
                ys = [(sb(es, "nys%d" % i, [128, 512], BF16), Buf()) for i in range(2)]
                yk = 0
                pk = 0
                for h in range(4):
                    S.dma("sp", kT[:], PF[R_NAK + h * 128:R_NAK + (h + 1) * 128, :], reads=[bPF], writes=[bk_])
                    S.dma("sp", qT[:], PF[R_NAQ + h * 128:R_NAQ + (h + 1) * 128, :], reads=[bPF], writes=[bq_])
                    S.dma("sp", Vt[:], VTv[:, :, h * 128:(h + 1) * 128], reads=[bVT], writes=[bv_])
                    S.dma("sp", BT[:], nabt_d[l, h], writes=[bbt])
                    S.cp("dve", BTb[:], BT[:], [bbt], [bbtb])
                    for g0 in range(0, len(rows), 8):
                        grp = rows[g0:g0 + 8]
                        bo, bbo = S.bankg('o', [0, 1])
                        bd, bbd = S.bankg('d', [2, 3])
                        pend = None
                        for si, rw in enumerate(grp + [None]):
                            cur = None
                            if rw is not None:
                                q0, tl = rw
                                bs, bbs = S.bankg('s', [4, 5, 6, 7])
                                for i, (j, idx) in enumerate(tl):
                                    lastt = (i == len(tl) - 1)
                                    S.mm(PS(bs, 64, i * 64), kT[:, j * 128:(j + 1) * 128], qT[:, q0:q0 + 64], True, idx is None,
                                         [bk_, bq_], [bbs], sig=(lastt and idx is None))
                                    if idx is not None:
                                        S.mm(PS(bs, 64, i * 64), ident_b, BTb[:, idx, :], False, True, [bC, bbtb], [bbs], sig=lastt)
                                p_t, p_b = pT[pk % 4]; pk += 1
                                nt = len(tl)
                                S.act(p_t[:, :nt * 64], PS(bs, nt * 64), AF.Exp, [bbs], [p_b])
                                cur = (tl, p_t, p_b, si)
                            if pend is not None:
                                tl_, p_t_, p_b_, si_ = pend
                                for i, (j, idx) in enumerate(tl_):
                                    S.mm(PS(bo, 64, si_ * 64), Vt[:, j, :], p_t_[:, i * 64:(i + 1) * 64], i == 0, i == len(tl_) - 1,
                                         [bv_, p_b_], [bbo], sig=False)
                                for i, (j, idx) in enumerate(tl_):
                                    S.mm(PS(bd, 64, si_ * 64), ones_b, p_t_[:, i * 64:(i + 1) * 64], i == 0, i == len(tl_) - 1,
                                         [bC, p_b_], [bbd], sig=(i == len(tl_) - 1))
                            pend = cur
                        w = len(grp) * 64
                        q00 = grp[0][0]
                        S.op("dve", lambda e, w=w, bd=bd: e.reciprocal(rd[:, :w], PS(bd, w)), [bbd], [brd])
                        y_t, y_b = ys[yk % 2]; yk += 1
                        S.tt("dve", y_t[:, :w], PS(bo, w), rd[:, :w], ALU.mult, [bbo, brd], [y_b])
                        S.dma("sp", YT[h * 128:(h + 1) * 128, q00:q00 + w], y_t[:, :w], reads=[y_b], writes=[bYT])
                S.barrier()

        def conv_stage(l):
            NP = NTOK + 6
            with ExitStack() as es:
                xin = [(sb(es, "cx%d" % i, [128, NP], BF16), Buf()) for i in range(2)]
                acc = [(sb(es, "ca%d" % i, [128, NP], F32), Buf()) for i in range(2)]
                yo = [(sb(es, "cy%d" % i, [128, NP], BF16), Buf()) for i in range(2)]
                for i in range(2):
                    S.op("pool", lambda g, i=i: g.memset(xin[i][0][:], 0.0), [], [xin[i][1]])
                for c in range(16):
                    x_t, x_b = xin[c % 2]; a_t, a_b = acc[c % 2]; y_t, y_b = yo[c % 2]
                    S.dma("sp", x_t[:, 1:1 + T], QKR[c * 128:(c + 1) * 128, 0:T], reads=[bQKR], writes=[x_b])
                    S.dma("sp", x_t[:, T + 3:T + 3 + C], QKR[c * 128:(c + 1) * 128, T:NTOK], reads=[bQKR], writes=[x_b])
                    wc = lambda k: vec[:, V_CW + l * 48 + c * 3 + k:V_CW + l * 48 + c * 3 + k + 1]
                    n_ = NP - 2
                    S.act(a_t[:, 0:n_], x_t[:, 0:n_], AF.Identity, [x_b, bC], [a_b], scale=wc(0))
                    S.stt(a_t[:, 0:n_], x_t[:, 1:1 + n_], wc(1), a_t[:, 0:n_], ALU.mult, ALU.add, [x_b, a_b, bC], [a_b])
                    S.stt(a_t[:, 0:n_], x_t[:, 2:2 + n_], wc(2), a_t[:, 0:n_], ALU.mult, ALU.add, [x_b, a_b, bC], [a_b])
                    S.act(y_t[:, 0:n_], a_t[:, 0:n_], AF.Silu, [a_b, bC], [y_b], bias=vec[:, V_CB + l * 16 + c:V_CB + l * 16 + c + 1])
                    if c >= 8:
                        S.act(y_t[:, 0:n_], y_t[:, 0:n_], AF.Identity, [y_b], [y_b], scale=SC)
                    row = R_MQ + c * 128
                    S.dma("sp", PF[row:row + 128, 0:T], y_t[:, 0:T], reads=[y_b], writes=[bPF])
                    S.dma("sp", PF[row:row + 128, T:NTOK], y_t[:, T + 2:T + 2 + C], reads=[y_b], writes=[bPF])
                S.barrier()

        def mlstm_stage(l, GI, GF, bG, need_ctx=True):
            cf_order = [32, 33] + list(range(32))
            cb_order = [33, 32] + list(range(31, -1, -1))
            sel = lambda dh: cf[0:40, CF_SEL + dh * 128:CF_SEL + (dh + 1) * 128]
            with ExitStack() as es:
                NEGM = sb(es, "NEGM", [40, NCH, 128], F32)
                NEGm = sb(es, "NEGm", [40, NCH, 128], F32)
                Ucol = sb(es, "Ucol", [128, NCH, 40], F32)
                Wcol = sb(es, "Wcol", [128, NCH, 40], F32)
                CSb = sb(es, "CSb", [128, 16, 3 * NCH], F32)
                bR = Buf()
                G3 = GI[0:40, :].rearrange("p (c l) -> p c l", l=128)
                F3 = GF[0:40, :].rearrange("p (c l) -> p c l", l=128)
                with ExitStack() as pes:
                    tA = sb(pes, "mtA", [40, NCH, 128], F32)
                    tB = sb(pes, "mtB", [40, NCH, 128], F32)
                    tC = sb(pes, "mtC", [40, NCH, 128], F32)
                    sm = sb(pes, "msm", [40, 8, NCH], F32)
                    CHS = sb(pes, "mCHS", [40, 3, NCH], F32)
                    bP = bG

                    def dscan(x0, bufs, op):
                        X = x0
                        k = 0
                        for s_ in (1, 2, 4, 8, 16, 32, 64):
                            Y = bufs[k % len(bufs)]; k += 1
                            S.cp("act", Y[:], X[:] if X is not x0 else x0, [bP], [bP])
                            Xv = X[:] if X is not x0 else x0
                            S.tt("dve", Y[0:8, :, s_:], Xv[0:8, :, s_:], Xv[0:8, :, :128 - s_], op, [bP], [bP])
                            S.tt("dve", Y[32:40, :, :128 - s_], Xv[32:40, :, :128 - s_], Xv[32:40, :, s_:], op, [bP], [bP])
                            X = Y
                        return X

                    S.op("pool", lambda g: g.memset(NEGM[:], 0.0), [], [bP])
                    S.op("pool", lambda g: g.memset(NEGm[:], 0.0), [], [bP])
                    Bt = dscan(F3, [tA, tC], ALU.add)
                    assert Bt is tA
                    S.tt("dve", G3, G3, tA[:], ALU.subtract, [bP], [bP])
                    CMt = dscan(G3, [tB, tC], ALU.max)
                    assert CMt is tB
                    for i in range(NCH):
                        S.cp("pool", sm[0:8, 0, i:i + 1], tA[0:8, cf_order[i], 127:128], [bP], [bP])
                        S.cp("pool", sm[32:40, 0, i:i + 1], tA[32:40, cb_order[i], 0:1], [bP], [bP])
                        S.cp("pool", sm[0:8, 1, i:i + 1], tB[0:8, cf_order[i], 127:128], [bP], [bP])
                        S.cp("pool", sm[32:40, 1, i:i + 1], tB[32:40, cb_order[i], 0:1], [bP], [bP])
                    S.op("dve", lambda g: g.memset(sm[:, 3, :], 0.0), [], [bP])
                    for i in range(NCH):
                        S.tt("dve", sm[:, 4, i:i + 1], sm[:, 3, i:i + 1], sm[:, 1, i:i + 1], ALU.max, [bP], [bP])
                        if i < NCH - 1:
                            S.tt("dve", sm[:, 3, i + 1:i + 2], sm[:, 4, i:i + 1], sm[:, 0, i:i + 1], ALU.add, [bP], [bP])
                    S.cp("dve", CHS[:, 0, :], sm[:, 3, :], [bP], [bP])
                    S.tt("dve", CHS[:, 1, :], sm[:, 3, :], sm[:, 4, :], ALU.subtract, [bP], [bP])
                    S.tt("dve", CHS[:, 2, :], sm[:, 1, :], sm[:, 4, :], ALU.subtract, [bP], [bP])
                    S.act(CHS[:, 1:3, :], CHS[:, 1:3, :], AF.Exp, [bP], [bP])
                    for i in range(NCH):
                        S.ts("dve", NEGM[0:8, cf_order[i], :], tB[0:8, cf_order[i], :], sm[0:8, 3, i:i + 1], None, ALU.max, None, [bP], [bP])
                        S.ts("dve", NEGM[32:40, cb_order[i], :], tB[32:40, cb_order[i], :], sm[32:40, 3, i:i + 1], None, ALU.max, None, [bP], [bP])
                    for (a, b_) in ((0, 8), (32, 40)):
                        S.tt("dve", NEGm[a:b_], tA[a:b_], NEGM[a:b_], ALU.add, [bP], [bP])
                        S.ts("dve", NEGm[a:b_], NEGm[a:b_], -1.0, None, ALU.mult, None, [bP], [bP])
                        S.ts("dve", NEGM[a:b_], NEGM[a:b_], -1.0, None, ALU.mult, None, [bP], [bP])
                    S.cp("pool", sm[0:8, 5, :], tB[0:8, :, 127], [bP], [bP])
                    S.cp("pool", sm[32:40, 5, :], tB[32:40, :, 0], [bP], [bP])
                    for (a, b_) in ((0, 8), (32, 40)):
                        S.tt("dve", tA[a:b_], G3[a:b_], sm[a:b_, 5, :].unsqueeze(2).to_broadcast([b_ - a, NCH, 128]), ALU.subtract, [bP], [bP])
                    S.act(tA[:], tA[:], AF.Exp, [bP], [bP])
                    for (src, dstc) in ((G3, Ucol), (tA[:], Wcol)):
                        for c0 in range(0, NCH, 12):
                            ncn = min(12, NCH - c0)
                            bk, bb = S.bank()
                            for cc in range(ncn):
                                S.mm(PS(bk, 40, cc * 40), src[:, c0 + cc, :], cf[0:40, CF_ID:CF_ID + 40], True, True, [bP, bC], [bb], sig=(cc == ncn - 1))
                            S.cp("dve", dstc[:, c0:c0 + ncn, :], PS(bk, ncn * 40).rearrange("p (c j) -> p c j", j=40), [bb], [bR])
                    for d0 in range(0, 16, 4):
                        bk, bb = S.bank()
                        for dd in range(4):
                            S.mm(PS(bk, 3 * NCH, dd * 3 * NCH), sel(d0 + dd), CHS[:].rearrange("p a c -> p (a c)"), True, True, [bP, bC], [bb], sig=(dd == 3))
                        S.cp("dve", CSb[:, d0:d0 + 4, :], PS(bk, 4 * 3 * NCH).rearrange("p (d n) -> p d n", n=3 * NCH), [bb], [bR])
                    S.cp("pool", tC[:], NEGM[:], [bP], [bR])
                    S.barrier()
                qT = sb(es, "mqT", [128, NTOK], BF16); bq_ = Buf()
                kT = sb(es, "mkT", [128, NTOK], BF16); bk_ = Buf()
                Vt = sb(es, "mV", [128, NCH, 129], BF16); bv_ = Buf()
                Kt = sb(es, "mKt", [128, NCH, 128], BF16); bkt = Buf()
                osg = GF[:].bitcast(BF16); bos = Buf()
                Hacc = GI; bH = Buf()
                Cst = [[sb(es, "mC%d_%d" % (d_, i), [128, 129], F32) for i in range(2)] for d_ in range(2)]
                bSt = [[Buf(), Buf()] for d_ in range(2)]
                CLs = [sb(es, "mCLs%d" % d_, [128, NCH, 129], F32) for d_ in range(2)]; bCL = [Buf(), Buf()]
                Call = [sb(es, "mCall%d" % d_, [128, NCH, 129], BF16) for d_ in range(2)]; bCa = [Buf(), Buf()]
                f32s = [[sb(es, "mf%d_%d" % (i, j), [128, 128], F32) for j in range(5)] for i in range(3)]
                b16s = [[sb(es, "mb%d_%d" % (i, j), [128, 128], BF16) for j in range(3)] for i in range(3)]
                bfs = [[Buf() for j in range(5)] for i in range(3)]
                bbs_ = [[Buf() for j in range(3)] for i in range(3)]
                sq = sb(es, "msq", [128, 512], BF16); bsq = Buf()
                rsn = sb(es, "mrs", [128, 2, 512], F32); brs = Buf()
                ys = [(sb(es, "mys%d" % i, [128, 512], BF16), Buf()) for i in range(2)]
                yk = 0
                it = 0
                for h in range(8):
                    S.dma("sp", qT[:], PF[R_MQ + h * 128:R_MQ + (h + 1) * 128, :], reads=[bPF], writes=[bq_])
                    S.dma("sp", kT[:], PF[R_MK + h * 128:R_MK + (h + 1) * 128, :], reads=[bPF], writes=[bk_])
                    S.dma("sp", osg[:, 0:NTOK], PF[R_MO + h * 128:R_MO + (h + 1) * 128, :], reads=[bPF], writes=[bos])
                    S.dma("sp", Vt[:, :, 0:128], VTv[:, :, 768 + h * 128:768 + (h + 1) * 128], reads=[bVT], writes=[bv_])
                    S.op("pool", lambda g: g.memset(Vt[:, :, 128:129], 1.0), [], [bv_])
                    for c0 in range(0, NCH, 8):
                        ncn = min(8, NCH - c0)
                        bk, bb = S.bankg("mD", [6, 7])
                        pst = ps[:, bk, :].bitcast(BF16)
                        for cc in range(ncn):
                            S.op("pe", lambda e, cc=cc, c0=c0, pst=pst: e.transpose(pst[:, cc * 128:(cc + 1) * 128], kT[:, (c0 + cc) * 128:(c0 + cc + 1) * 128], ident_b),
                                 [bk_, bC], [bb], sig=(cc == ncn - 1))
                        S.cp("act", Kt[:, c0:c0 + ncn, :], pst[:, 0:ncn * 128].rearrange("p (c d) -> p c d", d=128), [bb], [bkt])
                    for dr in range(2):
                        dh = dr * 8 + h
                        row = h + 32 * dr
                        order = cf_order if dr == 0 else cb_order
                        pos = {c: i for i, c in enumerate(order)}
                        msk = cf[:, CF_MSK + dr * 128:CF_MSK + (dr + 1) * 128]
                        for i, c in enumerate(order):
                            st_ = it % 3; it += 1
                            B16 = b16s[st_]; bB = bbs_[st_]
                            inp_ = CSb[:, dh, 2 * NCH + i:2 * NCH + i + 1]
                            S.act(B16[2][:], Kt[:, c, :], AF.Identity, [bkt, bR], [bB[2]], scale=Wcol[:, c, row:row + 1])
                            bCk, bbC = S.bankg("mC", [4, 5])
                            S.mm(PS(bCk, 129, 0), B16[2][:], Vt[:, c, :], True, True, [bB[2], bv_], [bbC])
                            S.ts("dve", CLs[dr][:, i, :], PS(bCk, 129, 0), inp_, None, ALU.mult, None, [bbC, bR], [bCL[dr]])
                    for dr in range(2):
                        S.op("pool", lambda g, dr=dr: g.memset(Cst[dr][0][:], 0.0), [], [bSt[dr][0]])
                        S.op("pool", lambda g, dr=dr: g.memset(Call[dr][:, 0, :], 0.0), [], [bCa[dr]])
                    for i in range(NCH - 1):
                        cur = i % 2; nxt = 1 - cur
                        for dr in range(2):
                            dec = CSb[:, dr * 8 + h, NCH + i:NCH + i + 1]
                            S.stt(Cst[dr][nxt][:], Cst[dr][cur][:], dec, CLs[dr][:, i, :], ALU.mult, ALU.add,
                                  [bSt[dr][cur], bCL[dr], bR], [bSt[dr][nxt]])
                            S.cp("act", Call[dr][:, i + 1, :], Cst[dr][nxt][:], [bSt[dr][nxt]], [bCa[dr]])
                    for dr in range(2):
                        dh = dr * 8 + h
                        row = h + 32 * dr
                        order = cf_order if dr == 0 else cb_order
                        pos = {c: i for i, c in enumerate(order)}
                        msk = cf[:, CF_MSK + dr * 128:CF_MSK + (dr + 1) * 128]
                        groups = [(g0, 4) for g0 in range(0, 32, 4)] + ([(32, 2)] if need_ctx else [])
                        clist = []
                        for (g0, gn) in groups:
                            for j in range(gn):
                                clist.append((g0, gn, j))
                        gb = {}

                        def front(g0, gn, j, st_):
                            F = f32s[st_]; B16 = b16s[st_]; bF = bfs[st_]; bB = bbs_[st_]
                            c = g0 + j
                            i = pos[c]
                            tok = c * 128
                            if j == 0:
                                b1, bb1 = S.bankg("mR1", [0, 1])
                                b2, bb2 = S.bankg("mR2", [2, 3])
                                b3, bb3 = S.bankg("mS", [4, 5])
                                S.mm(PS(b1, gn * 128), sel(dh), NEGM[:, g0:g0 + gn, :], True, True, [bR, bC], [bb1])
                                S.mm(PS(b2, gn * 128), sel(dh), NEGm[:, g0:g0 + gn, :], True, True, [bR, bC], [bb2])
                                for jj in range(gn):
                                    t2 = (g0 + jj) * 128
                                    S.mm(PS(b3, 128, jj * 128), kT[:, t2:t2 + 128], qT[:, t2:t2 + 128], True, True, [bk_, bq_], [bb3], sig=(jj == gn - 1))
                                gb[g0] = (b1, bb1, b2, bb2, b3, bb3)
                            b1, bb1, b2, bb2, b3, bb3 = gb[g0]
                            S.act(F[0][:], PS(b1, 128, j * 128), AF.Exp, [bb1, bR], [bF[0]], bias=Ucol[:, c, row:row + 1])
                            S.act(F[1][:], PS(b1, 128, j * 128), AF.Exp, [bb1, bR], [bF[1]], bias=CSb[:, dh, i:i + 1])
                            S.act(F[2][:], PS(b2, 128, j * 128), AF.Exp, [bb2], [bF[2]])
                            S.tt("pool", F[0][:], F[0][:], msk, ALU.mult, [bF[0], bC], [bF[0]])
                            S.tt("dve", B16[0][:], PS(b3, 128, j * 128), F[0][:], ALU.mult, [bb3, bF[0]], [bB[0]])
                            S.tt("pool", B16[1][:], qT[:, tok:tok + 128], F[1][:], ALU.mult, [bq_, bF[1]], [bB[1]])
                            S.cp("pool", B16[2][:], Call[dr][:, i, 128:129].to_broadcast([128, 128]), [bCa[dr]], [bB[2]])

                        def back(g0, gn, j, st_):
                            F = f32s[st_]; B16 = b16s[st_]; bF = bfs[st_]; bB = bbs_[st_]
                            c = g0 + j
                            i = pos[c]
                            tok = c * 128
                            bBk, bbB = S.bankg("mB", [6, 7])
                            S.mm(PS(bBk, 128, 0), Vt[:, c, 0:128], B16[0][:], True, False, [bv_, bB[0]], [bbB], sig=False)
                            S.mm(PS(bBk, 128, 0), Call[dr][:, i, 0:128], B16[1][:], False, True, [bCa[dr], bB[1]], [bbB], sig=False)
                            S.mm(PS(bBk, 128, 128), ones_b, B16[0][:], True, False, [bC, bB[0]], [bbB], sig=False)
                            S.mm(PS(bBk, 128, 128), B16[2][:], B16[1][:], False, True, [bB[2], bB[1]], [bbB])
                            S.act(F[3][:], PS(bBk, 128, 128), AF.Abs, [bbB], [bF[3]])
                            S.tt("dve", F[3][:], F[3][:], F[2][:], ALU.max, [bF[3], bF[2]], [bF[3]])
                            S.op("dve", lambda g, F=F: g.reciprocal(F[3][:], F[3][:]), [bF[3]], [bF[3]])
                            if dr == 0:
                                S.tt("dve", Hacc[:, tok:tok + 128], PS(bBk, 128, 0), F[3][:], ALU.mult, [bbB, bF[3]], [bH])
                            else:
                                S.tt("dve", F[4][:], PS(bBk, 128, 0), F[3][:], ALU.mult, [bbB, bF[3]], [bF[4]])
                                S.tt("pool", Hacc[:, tok:tok + 128], Hacc[:, tok:tok + 128], F[4][:], ALU.add, [bH, bF[4]], [bH])

                        sets = []
                        for n_, (g0, gn, j) in enumerate(clist):
                            st_ = it % 3; it += 1
                            sets.append(st_)
                            front(g0, gn, j, st_)
                            if n_ > 0:
                                back(*clist[n_ - 1], sets[n_ - 1])
                        back(*clist[-1], sets[-1])
                    for (o, w) in blocks(NTOK if need_ctx else T):
                        S.act(sq[:, :w], Hacc[:, o:o + w], AF.Square, [bH], [bsq])
                        bk, bb = S.bankg("mD", [6, 7])
                        S.mm(PS(bk, w), av2_b, sq[:, :w], True, True, [bsq, bC], [bb])
                        S.act(rsn[:, 0, :w], PS(bk, w), AF.Sqrt, [bb, bC], [brs], bias=smallv[:, 0:1])
                        S.op("dve", lambda g, w=w: g.reciprocal(rsn[:, 0, :w], rsn[:, 0, :w]), [brs], [brs])
                        S.stt(rsn[:, 1, :w], Hacc[:, o:o + w], vec[:, V_ON + l * 8 + h:V_ON + l * 8 + h + 1], rsn[:, 0, :w], ALU.mult, ALU.mult, [bH, brs, bC], [brs])
                        y_t, y_b = ys[yk % 2]; yk += 1
                        S.tt("dve", y_t[:, :w], rsn[:, 1, :w], osg[:, o:o + w], ALU.mult, [brs, bos], [y_b])
                        S.dma("sp", YT[1024 + h * 128:1024 + (h + 1) * 128, o:o + w], y_t[:, :w], reads=[y_b], writes=[bYT])
                S.barrier()

        def wout_stage(l, tiles, split=False):
            wo = w_out[l].rearrange("(c p) n -> p c n", p=128)
            YTv = YT.rearrange("(c p) t -> p c t", p=128)
            m0 = vec[:, V_M:V_M + 1]
            m1 = vec[:, V_M + 1:V_M + 2]
            with ExitStack() as es:
                xs = sb(es, "oxs", [128, 16, 1024], F32); bxs = Buf()
                yT = sb(es, "oyT", [128, 16, 1024], BF16); byT = Buf()
                wos = [(sb(es, "owo%d" % i, [128, 16, 512], BF16), Buf()) for i in range(2)]
                if split:
                    xB = [(sb(es, "oxB%d" % i, [128, 4, 1024], F32), Buf()) for i in range(2)]
                    yB = [(sb(es, "oyB%d" % i, [128, 4, 1024], BF16), Buf()) for i in range(2)]
                loads = []
                for _ in tiles:
                    for dp in range(4):
                        loads.append(lambda slot, dp=dp: S.dma("pool", slot[0][:], wo[:, :, dp * 512:(dp + 1) * 512], writes=[slot[1]]))
                wst = Stream(wos, loads)
                wi = 0
                bk_ = 0
                for (t0, n, s) in tiles:
                    blks = blocks(n)
                    for c4 in range(4):
                        cs_ = slice(c4 * 4, c4 * 4 + 4)
                        S.dma("sp", xs[:, cs_, :n], XTv[:, cs_, t0:t0 + n], reads=[bXT], writes=[bxs])
                        S.dma("sp", yT[:, cs_, :n], YTv[:, cs_, t0:t0 + n], reads=[bYT], writes=[byT])
                        if split:
                            xb_t, xb_b = xB[bk_ % 2]; yb_t, yb_b = yB[bk_ % 2]; bk_ += 1
                            S.dma("sp", xb_t[:, :, :n], XTv[:, cs_, TL + t0:TL + t0 + n], reads=[bXT], writes=[xb_b])
                            S.dma("sp", yb_t[:, :, :n], YTv[:, cs_, TL + t0:TL + t0 + n], reads=[bYT], writes=[yb_b])
                            S.act(xs[:, cs_, :n], xs[:, cs_, :n], AF.Identity, [bxs, bC], [bxs], scale=m0)
                            S.stt(xs[:, cs_, :n], xb_t[:, :, :n], m1, xs[:, cs_, :n], ALU.mult, ALU.add, [xb_b, bxs, bC], [bxs])
                            S.act(yT[:, cs_, :n], yT[:, cs_, :n], AF.Identity, [byT, bC], [byT], scale=m0)
                            S.stt(yT[:, cs_, :n], yb_t[:, :, :n], m1, yT[:, cs_, :n], ALU.mult, ALU.add, [yb_b, byT, bC], [byT])
                    for dp in range(4):
                        wt, wb = wst.get(wi); wi += 1
                        for dc in range(4):
                            d = dp * 4 + dc
                            bkl = [S.bank() for _ in blks]
                            for c in range(16):
                                for bi, (o, w) in enumerate(blks):
                                    S.mm(PS(bkl[bi][0], w), wt[:, c, dc * 128:(dc + 1) * 128], yT[:, c, o:o + w], c == 0, c == 15,
                                         [wb, byT], [bkl[bi][1]], sig=(c == 15 and bi == len(blks) - 1))
                            for bi, (o, w) in enumerate(blks):
                                S.stt(xs[:, d, o:o + w], PS(bkl[bi][0], w), AB[:, l, 1, 2, d, s:s + 1], xs[:, d, o:o + w],
                                      ALU.mult, ALU.add, [bkl[bi][1], bxs, bC], [bxs])
                    for c4 in range(4):
                        cs_ = slice(c4 * 4, c4 * 4 + 4)
                        if split:
                            S.dma("sp", XOv[:, cs_, t0:t0 + n], xs[:, cs_, :n], reads=[bxs], writes=[bXO])
                        else:
                            S.dma("sp", XTv[:, cs_, t0:t0 + n], xs[:, cs_, :n], reads=[bxs], writes=[bXT])
                S.barrier()

        for l in range(nlayers):
            last = (l == 1)
            ffn_stage(l, 0)
            if stop == "ffn1_%d" % l:
                dump_stage(XT, D)
                return nc
            mixnorm_stage(l)
            with ExitStack() as mes:
                GI = sb(mes, "GI", [128, NTOK], F32)
                GF = sb(mes, "GF", [128, NTOK], F32)
                bG = Buf()
                proj_stage(l, GI, GF, bG)
                vproj_stage(l)
                if stop == "proj_%d" % l:
                    dump_bf16(PF, 2048)
                    return nc
                gqa_stage(l, not last, own=last)
                if stop == "gqa_%d" % l:
                    dump_bf16(YT, 2048)
                    return nc
                na_stage(l, not last)
                if stop == "na_%d" % l:
                    dump_bf16(YT, 2048)
                    return nc
                conv_stage(l)
                if stop == "conv_%d" % l:
                    dump_bf16(PF[R_MQ:R_MQ + 2048], 2048)
                    return nc
                mlstm_stage(l, GI, GF, bG, need_ctx=not last)
                if stop == "ml_%d" % l:
                    dump_bf16(YT, 2048)
                    return nc
                if stop == "mlc_%d" % l:
                    dump_bf16(YT, 128, tok0=NTOK - T)
                    return nc
            if last:
                wout_stage(l, OWN, split=True)
            else:
                wout_stage(l, TILES)
            if stop == "mix_%d" % l:
                dump_stage(XT, D)
                return nc
            if last:
                ffn_stage(l, 1, OWN, xv=XOv, bxv=bXO)
            else:
                ffn_stage(l, 1, TILES)
        final_stage()
    return nc


def _consts():
    cf = np.zeros((128, NCF), np.float32)
    cf[:, CF_ID:CF_ID + 128] = np.eye(128, dtype=np.float32)
    R = np.zeros((128, 128), np.float32)
    for j in range(128):
        h = (j // 64) * 64
        jj = j % 64
        if jj < 32:
            R[h + jj + 32, j] = -1.0
        else:
            R[h + jj - 32, j] = 1.0
    cf[:, CF_ROT:CF_ROT + 128] = R
    s = np.arange(128)[:, None]; q = np.arange(128)[None, :]
    cf[:, CF_MSK:CF_MSK + 128] = (s <= q).astype(np.float32)
    cf[:, CF_MSK + 128:CF_MSK + 256] = (s >= q).astype(np.float32)
    for dh in range(16):
        row = (dh % 8) + (32 if dh >= 8 else 0)
        cf[row, CF_SEL + dh * 128:CF_SEL + (dh + 1) * 128] = 1.0
    cb = np.zeros((128, NCB), np.float32)
    cb[:, CB_ONE:CB_ONE + 128] = 1.0
    cb[:, CB_AVG:CB_AVG + 128] = 1.0 / 2048
    cb[:, CB_AV2:CB_AV2 + 128] = 1.0 / 128
    cb[:, CB_ID:CB_ID + 128] = np.eye(128)
    t = np.arange(T)
    row = (t // 64).astype(np.float32); col = (t % 64).astype(np.float32)
    half = 64
    inv = (1.0 / (np.float32(10000.0) ** (np.arange(0, half, 2, dtype=np.float32) / np.float32(half)))).astype(np.float32)
    ar = row[:, None] * inv[None, :]; ac = col[:, None] * inv[None, :]
    ang = np.concatenate([ar, ar, ac, ac], -1)
    rope = np.stack([np.cos(ang).T, np.sin(ang).T]).astype(np.float32)
    return cf, cb.astype(ml_dtypes.bfloat16), np.ascontiguousarray(rope)


def _nabt(rpb):
    L = rpb.shape[0]
    out = np.full((L, 4, 128, 16, 64), NEG, np.float32)
    qc = np.arange(64)
    cs = np.clip(qc - 8, 0, 48)
    kc = np.arange(64)
    col_in = (kc[None, :] >= cs[:, None]) & (kc[None, :] < cs[:, None] + 16)
    dc = np.clip(kc[None, :] - qc[:, None], -15, 15) + 15
    for idx in range(16):
        if idx < 14:
            d0, valid = idx - 7, (True, True)
        elif idx == 14:
            d0, valid = -5, (False, True)
        else:
            d0, valid = 3, (True, False)
        for rr in range(2):
            if not valid[rr]:
                continue
            dr = d0 + rr + 7
            if dr < 0 or dr > 14:
                continue
            vals = rpb[:, :, dr, :][:, :, dc]
            vals = np.where(col_in[None, None], vals, np.float32(NEG))
            out[:, :, rr * 64:(rr + 1) * 64, idx, :] = vals.transpose(0, 1, 3, 2)
    return out


def _vec(b, r, inp):
    v = np.zeros((128, NV), np.float32)
    fm = lambda a: np.asarray(a, np.float32).reshape(-1, 128).T
    cc = np.stack([fm(inp["c"][b]), fm(inp["c_ctx"])], -1)
    v[:, V_CC:V_CC + 32] = cc.reshape(128, 32)
    for l in range(2):
        v[:, V_BADA + l * 144:V_BADA + (l + 1) * 144] = fm(inp["b_ada"][l])
        for si, nm in enumerate(("norm_ff1", "norm_mix", "norm_ff2")):
            v[:, V_GAIN + (l * 3 + si) * 16:V_GAIN + (l * 3 + si) * 16 + 16] = fm(inp[nm][l])
        cw = np.stack([fm(inp["ml_conv_w"][l][k]) for k in range(3)], -1)
        v[:, V_CW + l * 48:V_CW + (l + 1) * 48] = cw.reshape(128, 48)
        v[:, V_CB + l * 16:V_CB + (l + 1) * 16] = fm(inp["ml_conv_b"][l])
        v[:, V_QN + l] = inp["gqa_q_norm"][l]
        v[:, V_KN + l] = inp["gqa_k_norm"][l]
        v[:, V_ON + l * 8:V_ON + (l + 1) * 8] = fm(inp["ml_out_norm"][l])
        gb = np.asarray(inp["ml_gate_b"][l], np.float32).reshape(4, 8)
        v[0:8, V_GBI + l] = gb[0]; v[32:40, V_GBI + l] = gb[2]
        v[0:8, V_GBF + l] = gb[1]; v[32:40, V_GBF + l] = gb[3]
    v[:, V_FIN:V_FIN + 16] = fm(inp["final_norm"])
    v[:, V_M + r] = 1.0
    return v


def _run(inp, stop=None, nlayers=2, ncores=NCORES):
    inp = {k: np.asarray(v) for k, v in inp.items()}
    cf, cb, rope = _consts()
    nabt = _nabt(np.asarray(inp["na_rpb"], np.float32))
    nc = build(stop=stop, nlayers=nlayers)
    shared = {"cf": cf, "cb": cb, "rope": rope, "nabt": nabt}
    for nm in ("w_ada", "ff1_gate", "ff1_up", "ff1_down", "ff2_gate", "ff2_up", "ff2_down", "w_in", "w_out"):
        shared[nm] = np.ascontiguousarray(inp[nm], dtype=np.float32)
    in_maps = []
    xts = {}
    for c in range(ncores):
        b, r = c // 2, c % 2
        m = dict(shared)
        if b not in xts:
            xts[b] = np.ascontiguousarray(np.concatenate([inp["x"][b].T, inp["ctx"][b].T], axis=1), dtype=np.float32)
        m["xT0"] = xts[b]
        m["vec"] = _vec(b, r, inp)
        in_maps.append(m)
    res = run_bass_kernel_spmd(nc, in_maps, core_ids=list(range(ncores)))
    return [np.asarray(r["outT"]) for r in res.results]


def kernel(**inputs):
    outs = _run(inputs)
    return np.stack([np.concatenate([outs[2 * b].T, outs[2 * b + 1].T], axis=0) for b in range(len(outs) // 2)]).astype(np.float32)
```

```python
from contextlib import ExitStack
import numpy as np
import ml_dtypes
import concourse.bass as bass
import concourse.mybir as mybir
from concourse.bass_utils import run_bass_kernel_spmd

F32 = mybir.dt.float32
BF16 = mybir.dt.bfloat16
AF = mybir.ActivationFunctionType
ALU = mybir.AluOpType

D = 2048; T = 4096; C = 256; NTOK = T + C; DFF = 5632; INW = 6688; KC = 16
NCH = NTOK // 128
EPS = 1e-6
NEG = -30000.0
SC = 128 ** -0.5
NCORES = 8
TL = 2048
_DBG_TILES = None

V_CC = 0
V_BADA = V_CC + 32
V_GAIN = V_BADA + 288
V_FIN = V_GAIN + 96
V_CW = V_FIN + 16
V_CB = V_CW + 96
V_QN = V_CB + 32
V_KN = V_QN + 2
V_ON = V_KN + 2
V_GBI = V_ON + 16
V_GBF = V_GBI + 2
V_M = V_GBF + 2
NV = V_M + 2
CF_ID = 0
CF_ROT = CF_ID + 128
CF_MSK = CF_ROT + 128
CF_SEL = CF_MSK + 256
NCF = CF_SEL + 16 * 128
CB_ONE = 0
CB_AVG = 128
CB_AV2 = 256
CB_ID = 384
NCB = 512


class Buf:
    __slots__ = ("w", "r")

    def __init__(self):
        self.w = None
        self.r = {}


class Sched:
    def __init__(self, nc, ndma=48):
        self.nc = nc
        self.eng = {"pe": nc.tensor, "act": nc.scalar, "dve": nc.vector, "pool": nc.gpsimd, "sp": nc.sync}
        self.sems = []
        self.esem = {}
        self.ecnt = {}
        for e in ("pe", "act", "dve", "pool"):
            self.esem[e] = len(self.sems)
            self.sems.append(nc.alloc_semaphore("s_" + e))
            self.ecnt[e] = 0
        self.dsem = []
        self.duse = []
        for i in range(ndma):
            self.dsem.append(len(self.sems))
            self.sems.append(nc.alloc_semaphore("d%d" % i))
            self.duse.append(0)
        self.dk = 0
        self.seen = {e: {} for e in self.eng}
        self.nb = 0
        self.banks = [Buf() for _ in range(8)]
        self.gc = {}

    def bank(self):
        b = self.nb % 8
        self.nb += 1
        return b, self.banks[b]

    def bankg(self, key, banks):
        c = self.gc.get(key, 0)
        self.gc[key] = c + 1
        b = banks[c % len(banks)]
        return b, self.banks[b]

    def _wait(self, e, toks):
        need = {}
        own = self.esem.get(e, -1) if e == "pe" else -1
        for t in toks:
            if t is None:
                continue
            si, v = t
            if si == own:
                continue
            if self.seen[e].get(si, 0) >= v:
                continue
            if need.get(si, 0) < v:
                need[si] = v
        for si, v in need.items():
            self.eng[e].wait_ge(self.sems[si], v)
            self.seen[e][si] = v

    @staticmethod
    def _toks(reads, writes):
        toks = [b.w for b in reads]
        for b in writes:
            toks.append(b.w)
            toks.extend(b.r.items())
        return toks

    @staticmethod
    def _upd(tok, reads, writes):
        for b in writes:
            b.w = tok
            b.r = {}
        for b in reads:
            if b.r.get(tok[0], 0) < tok[1]:
                b.r[tok[0]] = tok[1]

    def op(self, e, fn, reads=(), writes=(), sig=True):
        self._wait(e, self._toks(reads, writes))
        ins = fn(self.eng[e])
        si = self.esem[e]
        if sig:
            ins.then_inc(self.sems[si], 1)
            self.ecnt[e] += 1
            tok = (si, self.ecnt[e])
        else:
            tok = (si, self.ecnt[e] + 1)
        self._upd(tok, reads, writes)

    def dma(self, q, out, in_, reads=(), writes=()):
        toks = self._toks(reads, writes)
        s = self.dk % len(self.dsem)
        self.dk += 1
        si = self.dsem[s]
        if self.duse[s] > 0:
            toks.append((si, 16 * self.duse[s]))
        self._wait(q, toks)
        self.eng[q].dma_start(out=out, in_=in_).then_inc(self.sems[si], 16)
        self.duse[s] += 1
        self._upd((si, 16 * self.duse[s]), reads, writes)

    def barrier(self):
        toks = [(self.esem[e], self.ecnt[e]) for e in self.esem]
        toks += [(self.dsem[s], 16 * self.duse[s]) for s in range(len(self.dsem)) if self.duse[s]]
        for e in self.eng:
            self._wait(e, toks)

    def mm(self, out, lhsT, rhs, start, stop, reads, writes, sig=True):
        self.op("pe", lambda e: e.matmul(out, lhsT, rhs, start=start, stop=stop), reads, writes, sig)

    def act(self, out, in_, func, reads, writes, bias=0.0, scale=1.0):
        self.op("act", lambda e: e.activation(out=out, in_=in_, func=func, bias=bias, scale=scale), reads, writes)

    def tt(self, e, out, in0, in1, op, reads, writes):
        self.op(e, lambda g: g.tensor_tensor(out, in0, in1, op), reads, writes)

    def ts(self, e, out, in0, s1, s2, op0, op1, reads, writes):
        if s2 is None:
            self.op(e, lambda g: g.tensor_scalar(out, in0, s1, None, op0), reads, writes)
        else:
            self.op(e, lambda g: g.tensor_scalar(out, in0, s1, s2, op0, op1), reads, writes)

    def stt(self, out, in0, scalar, in1, op0, op1, reads, writes):
        self.op("dve", lambda g: g.scalar_tensor_tensor(out, in0, scalar, in1, op0, op1), reads, writes)

    def cp(self, e, out, in_, reads, writes):
        if e == "act":
            self.op("act", lambda g: g.copy(out, in_), reads, writes)
        else:
            self.op(e, lambda g: g.tensor_copy(out, in_), reads, writes)


class Stream:
    def __init__(self, slots, loads):
        self.slots = slots
        self.loads = loads
        self.n = 0

    def ensure(self, i):
        i = min(i, len(self.loads) - 1)
        while self.n <= i:
            self.loads[self.n](self.slots[self.n % len(self.slots)])
            self.n += 1

    def get(self, i):
        self.ensure(i + len(self.slots) - 1)
        return self.slots[i % len(self.slots)]


def build(stop=None, nlayers=2):
    nc = bass.Bass("TRN2", target_bir_lowering=False)
    S = Sched(nc)
    din = lambda n, sh, dt=F32: nc.dram_tensor(n, list(sh), dt, kind="ExternalInput").ap()
    x0 = din("xT0", [D, NTOK])
    vec_d = din("vec", [128, NV])
    cf_d = din("cf", [128, NCF])
    cb_d = din("cb", [128, NCB], BF16)
    rope_d = din("rope", [2, 128, T])
    nabt_d = din("nabt", [2, 4, 128, 16, 64])
    w_ada = din("w_ada", [2, D, 9 * D])
    ffw = {}
    for nm in ("ff1_gate", "ff1_up", "ff2_gate", "ff2_up"):
        ffw[nm] = din(nm, [2, D, DFF])
    for nm in ("ff1_down", "ff2_down"):
        ffw[nm] = din(nm, [2, DFF, D])
    w_in = din("w_in", [2, D, INW])
    w_out = din("w_out", [2, D, D])
    out_d = nc.dram_tensor("outT", [D, TL], F32, kind="ExternalOutput").ap()
    dscr = lambda n, sh, dt: nc.dram_tensor(n, list(sh), dt, kind="Internal").ap()
    XT = dscr("XT", [D, NTOK], F32)
    HT = dscr("HT", [D, NTOK], BF16)
    PF = dscr("PF", [4864, NTOK], BF16)
    QKR = dscr("QKR", [2048, NTOK], BF16)
    VT = dscr("VT", [NTOK, 1792], BF16)
    YT = dscr("YT", [D, NTOK], BF16)
    XO = dscr("XO", [D, TL], F32)
    R_NAQ, R_NAK, R_GQ, R_GK, R_MQ, R_MK, R_MO = 0, 512, 1024, 1536, 1792, 2816, 3840
    bXT, bHT, bPF, bQKR, bVT, bYT = Buf(), Buf(), Buf(), Buf(), Buf(), Buf()
    bXO = Buf()

    ps = nc.alloc_psum_tensor("ps", [128, 8, 512], F32)

    def PS(b, w=512, o=0):
        return ps[:, b, o:o + w]

    TILES = _DBG_TILES or ([(i * 1024, 1024, 0) for i in range(4)] + [(T, C, 1)])

    def blocks(n):
        return [(o, min(512, n - o)) for o in range(0, n, 512)]

    with ExitStack() as top:
        _ctr = [0]

        def sb(es, n, sh, dt):
            _ctr[0] += 1
            return es.enter_context(nc.sbuf_tensor("%s_s%d" % (n, _ctr[0]), list(sh), dt))
        vec = sb(top, "vec", [128, NV], F32)
        cf = sb(top, "cf", [128, NCF], F32)
        cb = sb(top, "cb", [128, NCB], BF16)
        modT = sb(top, "modT", [128, 2, 144, 2], F32)
        AB = sb(top, "AB", [128, 2, 3, 3, 16, 2], F32)
        smallv = sb(top, "smallv", [128, 8], F32)
        bC = Buf()
        S.dma("sp", vec[:], vec_d, writes=[bC])
        S.dma("sp", cf[:], cf_d, writes=[bC])
        S.dma("sp", cb[:], cb_d, writes=[bC])
        ident = cf[:, CF_ID:CF_ID + 128]
        ones_b = cb[:, CB_ONE:CB_ONE + 128]
        avg_b = cb[:, CB_AVG:CB_AVG + 128]
        av2_b = cb[:, CB_AV2:CB_AV2 + 128]
        ident_b = cb[:, CB_ID:CB_ID + 128]

        with ExitStack() as es:
            xb = [sb(es, "x0b%d" % i, [128, 4, 1024], F32) for i in range(2)]
            xbb = [Buf(), Buf()]
            i = 0
            x0v = x0.rearrange("(c p) t -> p c t", p=128)
            XTv = XT.rearrange("(c p) t -> p c t", p=128)
            for (t0, n, s) in TILES:
                for c4 in range(4):
                    S.dma("sp", xb[i % 2][:, :, :n], x0v[:, c4 * 4:c4 * 4 + 4, t0:t0 + n], writes=[xbb[i % 2]])
                    S.dma("sp", XTv[:, c4 * 4:c4 * 4 + 4, t0:t0 + n], xb[i % 2][:, :, :n], reads=[xbb[i % 2]], writes=[bXT])
                    i += 1
            S.barrier()

        with ExitStack() as es:
            scT = sb(es, "scT", [128, 16, 2], BF16)
            wa = [sb(es, "wa%d" % i, [128, 16, 512], BF16) for i in range(3)]
            wab = [Buf() for _ in range(3)]
            S.act(scT[:], vec[:, V_CC:V_CC + 32].rearrange("p (k s) -> p k s", s=2), AF.Silu, [bC], [bC])
            for l in range(nlayers):
                wv = w_ada[l].rearrange("(k p) n -> p k n", p=128)
                loads = [(lambda slot, j=j: S.dma("pool", slot[0][:], wv[:, :, j * 512:(j + 1) * 512], writes=[slot[1]]))
                         for j in range(36)]
                st = Stream(list(zip(wa, wab)), loads)
                for j in range(36):
                    wt, wb = st.get(j)
                    bk, bb = S.bank()
                    for cc in range(4):
                        for k in range(16):
                            S.mm(PS(bk, 2, cc * 2), wt[:, k, cc * 128:(cc + 1) * 128], scT[:, k, :], k == 0, k == 15,
                                 [wb, bC], [bb], sig=(cc == 3 and k == 15))
                    for s in range(2):
                        S.tt("dve", modT[:, l, j * 4:(j + 1) * 4, s], ps[:, bk, s:8:2],
                             vec[:, V_BADA + l * 144 + j * 4:V_BADA + l * 144 + j * 4 + 4], ALU.add, [bb, bC], [bC])
                for sub in range(3):
                    gain = vec[:, V_GAIN + (l * 3 + sub) * 16:V_GAIN + (l * 3 + sub) * 16 + 16]
                    for s in range(2):
                        S.stt(AB[:, l, sub, 0, :, s], modT[:, l, (sub * 3 + 1) * 16:(sub * 3 + 2) * 16, s], 1.0, gain,
                              ALU.add, ALU.mult, [bC], [bC])
                        S.cp("dve", AB[:, l, sub, 1, :, s], modT[:, l, (sub * 3) * 16:(sub * 3 + 1) * 16, s], [bC], [bC])
                        S.ts("dve", AB[:, l, sub, 2, :, s], modT[:, l, (sub * 3 + 2) * 16:(sub * 3 + 3) * 16, s],
                             0.5 if sub != 1 else 1.0, None, ALU.mult, None, [bC], [bC])
            S.barrier()

        if stop == "ada":
            S.dma("sp", out_d[0:128, 0:576], modT[:].rearrange("p a b c -> p (a b c)"), reads=[bC], writes=[Buf()])
            S.barrier()
            return nc
        def norm_block(l, sub, s, xs, bxs, hT, bhT, o, w, sq, bsq, tmp, btmp, gainvec=None):
            S.act(sq[:, :, :w], xs[:, :, o:o + w], AF.Square, [bxs], [bsq])
            bk, bb = S.bank()
            for c in range(16):
                S.mm(PS(bk, w), avg_b, sq[:, c, :w], c == 0, c == 15, [bsq, bC], [bb], sig=(c == 15))
            S.act(tmp[:, 0, :w], PS(bk, w), AF.Sqrt, [bb], [btmp], bias=smallv[:, 0:1])
            S.op("dve", lambda g: g.reciprocal(tmp[:, 0, :w], tmp[:, 0, :w]), [btmp], [btmp])
            for c in range(16):
                if gainvec is None:
                    a_ap = AB[:, l, sub, 0, c, s:s + 1]
                    b_ap = AB[:, l, sub, 1, c, s:s + 1]
                else:
                    a_ap = gainvec[:, c:c + 1]
                    b_ap = None
                S.stt(tmp[:, 1 + c % 2, :w], xs[:, c, o:o + w], a_ap, tmp[:, 0, :w], ALU.mult, ALU.mult, [bxs, btmp, bC], [btmp])
                if b_ap is None:
                    S.cp("act", hT[:, c, o:o + w], tmp[:, 1 + c % 2, :w], [btmp], [bhT])
                else:
                    S.act(hT[:, c, o:o + w], tmp[:, 1 + c % 2, :w], AF.Identity, [btmp, bC], [bhT], bias=b_ap)

        S.op("dve", lambda g: g.memset(smallv[:, 0:1], EPS), [], [bC])
        S.op("dve", lambda g: g.memset(smallv[:, 3:4], 1.0), [], [bC])
        XTv = XT.rearrange("(c p) t -> p c t", p=128)
        XOv = XO.rearrange("(c p) t -> p c t", p=128)
        OWN = [(0, 1024, 0), (1024, 1024, 0)]
        FT = [(0, 1024, 0), (1024, 1024, 0), (2048, 1024, 0), (3072, 1024 + C, 0)]
        HTv = HT.rearrange("(c p) t -> p c t", p=128)

        def ffn_stage(l, which, tiles=None, xv=None, bxv=None):
            xv = XTv if xv is None else xv
            bxv = bXT if bxv is None else bxv
            sub = 0 if which == 0 else 2
            pre = "ff1" if which == 0 else "ff2"
            wg = ffw[pre + "_gate"][l].rearrange("(k p) n -> p k n", p=128)
            wu = ffw[pre + "_up"][l].rearrange("(k p) n -> p k n", p=128)
            wd = ffw[pre + "_down"][l].rearrange("(f p) n -> p f n", p=128)
            groups = [(g * 8, 8) for g in range(5)] + [(40, 4)]
            tiles = tiles or FT
            sblk = lambda t0, o: 1 if t0 + o >= T else 0
            with ExitStack() as es:
                NTM = max(n for (_, n, _) in tiles)
                xs = sb(es, "xs", [128, 16, NTM], F32); bxs = Buf()
                hT = sb(es, "hT", [128, 16, NTM], BF16); bhT = Buf()
                ab = sb(es, "ab", [128, 8, NTM], BF16); bab = Buf()
                sq = ab[:].rearrange("p a n -> p (a n)")[:, 0:8192].rearrange("p (c w) -> p c w", w=512)
                tmp = sb(es, "tmp", [128, 3, 512], F32); btmp = Buf()
                gsb = [sb(es, "gsb%d" % i, [128, 512], F32) for i in range(2)]; bgs = [Buf(), Buf()]
                gus = [(sb(es, "wg%d" % i, [128, 16, 256], BF16), sb(es, "wu%d" % i, [128, 16, 256], BF16), Buf()) for i in range(2)]
                wds = [(sb(es, "wd%d" % i, [128, 8, 256], BF16), Buf()) for i in range(2)]
                gu_loads, wd_loads = [], []
                for (t0, n, s) in tiles:
                    for (f0, nf) in groups:
                        for pc in range(nf // 2):
                            c0 = (f0 + pc * 2) * 128

                            def ld(slot, c0=c0):
                                S.dma("pool", slot[0][:], wg[:, :, c0:c0 + 256], writes=[slot[2]])
                                S.dma("pool", slot[1][:], wu[:, :, c0:c0 + 256], writes=[slot[2]])
                            gu_loads.append(ld)
                        for dp in range(8):
                            wd_loads.append(lambda slot, f0=f0, nf=nf, dp=dp: S.dma(
                                "pool", slot[0][:, :nf, :], wd[:, f0:f0 + nf, dp * 256:(dp + 1) * 256], writes=[slot[1]]))
                gst = Stream(gus, gu_loads)
                dst = Stream(wds, wd_loads)
                gi = 0
                di = 0
                gk = 0
                for (t0, n, s) in tiles:
                    blks = blocks(n)
                    for c4 in range(4):
                        S.dma("sp", xs[:, c4 * 4:c4 * 4 + 4, :n], xv[:, c4 * 4:c4 * 4 + 4, t0:t0 + n], reads=[bxv], writes=[bxs])
                    for (o, w) in blks:
                        norm_block(l, sub, sblk(t0, o), xs, bxs, hT, bhT, o, w, sq, bab, tmp, btmp)
                    for (f0, nf) in groups:
                        for pc in range(nf // 2):
                            wgt, wut, wb = gst.get(gi); gi += 1
                            for fc in range(2):
                                fi = pc * 2 + fc
                                bks = {}
                                for mi, wt in enumerate((wgt, wut)):
                                    bks[mi] = [S.bank() for _ in blks]
                                    for k in range(16):
                                        for bi, (o, w) in enumerate(blks):
                                            bk, bb = bks[mi][bi]
                                            S.mm(PS(bk, w), wt[:, k, fc * 128:(fc + 1) * 128], hT[:, k, o:o + w], k == 0, k == 15,
                                                 [wb, bhT], [bb], sig=(k == 15 and bi == len(blks) - 1))
                                for bi, (o, w) in enumerate(blks):
                                    g_ = gk % 2; gk += 1
                                    S.act(gsb[g_][:, :w], PS(bks[0][bi][0], w), AF.Silu, [bks[0][bi][1]], [bgs[g_]])
                                    S.tt("dve", ab[:, fi, o:o + w], gsb[g_][:, :w], PS(bks[1][bi][0], w), ALU.mult,
                                         [bgs[g_], bks[1][bi][1]], [bab])
                            if pc == 0:
                                dst.ensure(di + 1)
                        gst.ensure(gi + 1)
                        for dp in range(8):
                            wdt, wdb = dst.get(di); di += 1
                            for dc in range(2):
                                d = dp * 2 + dc
                                bkl = [S.bank() for _ in blks]
                                for fi in range(nf):
                                    for bi, (o, w) in enumerate(blks):
                                        bk, bb = bkl[bi]
                                        S.mm(PS(bk, w), wdt[:, fi, dc * 128:(dc + 1) * 128], ab[:, fi, o:o + w], fi == 0, fi == nf - 1,
                                             [wdb, bab], [bb], sig=(fi == nf - 1 and bi == len(blks) - 1))
                                for bi, (o, w) in enumerate(blks):
                                    bk, bb = bkl[bi]
                                    s_ = sblk(t0, o)
                                    S.stt(xs[:, d, o:o + w], PS(bk, w), AB[:, l, sub, 2, d, s_:s_ + 1], xs[:, d, o:o + w],
                                          ALU.mult, ALU.add, [bb, bxs, bC], [bxs])
                    for c4 in range(4):
                        S.dma("sp", xv[:, c4 * 4:c4 * 4 + 4, t0:t0 + n], xs[:, c4 * 4:c4 * 4 + 4, :n], reads=[bxs], writes=[bxv])
                S.barrier()

        def mixnorm_stage(l):
            with ExitStack() as es:
                xs = sb(es, "xs", [128, 16, 1024], F32); bxs = Buf()
                hT = sb(es, "hT", [128, 16, 1024], BF16); bhT = Buf()
                sq = sb(es, "sq", [128, 16, 512], BF16); bsq = Buf()
                tmp = sb(es, "tmp", [128, 3, 512], F32); btmp = Buf()
                for (t0, n, s) in TILES:
                    for c4 in range(4):
                        S.dma("sp", xs[:, c4 * 4:c4 * 4 + 4, :n], XTv[:, c4 * 4:c4 * 4 + 4, t0:t0 + n], reads=[bXT], writes=[bxs])
                    for (o, w) in blocks(n):
                        norm_block(l, 1, s, xs, bxs, hT, bhT, o, w, sq, bsq, tmp, btmp)
                    for c4 in range(4):
                        S.dma("sp", HTv[:, c4 * 4:c4 * 4 + 4, t0:t0 + n], hT[:, c4 * 4:c4 * 4 + 4, :n], reads=[bhT], writes=[bHT])
                S.barrier()

        def final_stage():
            outv = out_d.rearrange("(c p) t -> p c t", p=128)
            with ExitStack() as es:
                xs = sb(es, "xs", [128, 16, 1024], F32); bxs = Buf()
                ho = sb(es, "ho", [128, 16, 1024], F32); bho = Buf()
                sq = sb(es, "sq", [128, 16, 512], BF16); bsq = Buf()
                tmp = sb(es, "tmp", [128, 3, 512], F32); btmp = Buf()
                bout = Buf()
                for (t0, n, s) in OWN:
                    for c4 in range(4):
                        S.dma("sp", xs[:, c4 * 4:c4 * 4 + 4, :n], XOv[:, c4 * 4:c4 * 4 + 4, t0:t0 + n], reads=[bXO], writes=[bxs])
                    for (o, w) in blocks(n):
                        norm_block(0, 0, 0, xs, bxs, ho, bho, o, w, sq, bsq, tmp, btmp, gainvec=vec[:, V_FIN:V_FIN + 16])
                    for c4 in range(4):
                        S.dma("sp", outv[:, c4 * 4:c4 * 4 + 4, t0:t0 + n], ho[:, c4 * 4:c4 * 4 + 4, :n], reads=[bho], writes=[bout])
                S.barrier()

        def dump_stage(src, rows):
            with ExitStack() as es:
                xb = sb(es, "dmp", [128, TL], F32); bx = Buf()
                for c in range(rows // 128):
                    S.dma("sp", xb[:], src[c * 128:(c + 1) * 128, 0:TL], writes=[bx])
                    S.dma("sp", out_d[c * 128:(c + 1) * 128, :], xb[:], reads=[bx], writes=[Buf()])
                S.barrier()

        def dump_bf16(src, rows, tok0=0):
            with ExitStack() as es:
                xb = sb(es, "dmpb", [128, TL], BF16); bx = Buf()
                xf = sb(es, "dmpf", [128, TL], F32); bf = Buf()
                for c in range(rows // 128):
                    S.dma("sp", xb[:], src[c * 128:(c + 1) * 128, tok0:tok0 + TL], writes=[bx])
                    S.cp("dve", xf[:], xb[:], [bx], [bf])
                    S.dma("sp", out_d[c * 128:(c + 1) * 128, :], xf[:], reads=[bf], writes=[Buf()])
                S.barrier()

        PIECES = [(0, 4), (4, 4), (12, 4), (16, 2), (20, 4), (24, 4), (28, 4), (32, 4), (44, 4), (48, 4)]

        def chunk_info(ch):
            if ch < 4:
                return "scale", PF, R_NAQ + ch * 128
            if ch < 8:
                return "copy", PF, R_NAK + (ch - 4) * 128
            if ch < 16:
                return "q", PF, R_GQ + (ch - 12) * 128
            if ch < 18:
                return "k", PF, R_GK + (ch - 16) * 128
            if ch < 36:
                return "copy", QKR, (ch - 20) * 128
            return "sig", PF, R_MO + (ch - 44) * 128

        def proj_stage(l, GI, GF, bG):
            wv = w_in[l].rearrange("(k p) n -> p k n", p=128)
            with ExitStack() as es:
                hTs = [(sb(es, "phT%d" % i, [128, 16, 1024], BF16), Buf()) for i in range(2)]
                wps = [(sb(es, "pw%d" % i, [128, 16, 512], BF16), Buf()) for i in range(2)]
                stg = [(sb(es, "pst%d" % i, [128, 1024], BF16), Buf()) for i in range(3)]
                WI = sb(es, "WI", [128, 16, 40], BF16); WF = sb(es, "WF", [128, 16, 40], BF16); bW = Buf()
                sqb = sb(es, "psq", [128, 512], BF16); bsq = Buf()
                f32t = sb(es, "pf32", [128, 5, 512], F32); bft = [Buf() for _ in range(5)]
                cs = [(sb(es, "pcs%d" % i, [128, 2, 512], F32), Buf()) for i in range(2)]
                gtmp = sb(es, "gtmp", [40, 512], F32); bgt = Buf()
                S.op("dve", lambda g: g.memset(WI[:], 0.0), [], [bW])
                S.op("dve", lambda g: g.memset(WF[:], 0.0), [], [bW])
                for (dst, c0, o0) in ((WI, 6656, 0), (WF, 6664, 0), (WI, 6672, 32), (WF, 6680, 32)):
                    S.dma("pool", dst[:, :, o0:o0 + 8], wv[:, :, c0:c0 + 8], writes=[bW])
                S.ts("dve", smallv[:, 1:2], vec[:, V_QN + l:V_QN + l + 1], SC, None, ALU.mult, None, [bC], [bC])
                S.ts("dve", smallv[:, 2:3], vec[:, V_GBF + l:V_GBF + l + 1], -1.0, None, ALU.mult, None, [bC], [bC])
                ht_loads = []
                for (t0, n, s) in TILES:
                    def ldh(slot, t0=t0, n=n):
                        for c4 in range(4):
                            S.dma("sp", slot[0][:, c4 * 4:c4 * 4 + 4, :n], HTv[:, c4 * 4:c4 * 4 + 4, t0:t0 + n], reads=[bHT], writes=[slot[1]])
                    ht_loads.append(ldh)
                hst = Stream(hTs, ht_loads)
                w_loads = []
                for _ in TILES:
                    for (c0, ncn) in PIECES:
                        w_loads.append(lambda slot, c0=c0, ncn=ncn: S.dma(
                            "pool", slot[0][:, :, :ncn * 128], wv[:, :, c0 * 128:(c0 + ncn) * 128], writes=[slot[1]]))
                wst = Stream(wps, w_loads)
                wi = 0
                sk = 0
                for ti, (t0, n, s) in enumerate(TILES):
                    hT, bhT = hst.get(ti)
                    blks = blocks(n)
                    if s == 0:
                        for bi, (o, w) in enumerate(blks):
                            S.dma("sp", cs[bi][0][:, :, :w], rope_d[:, :, t0 + o:t0 + o + w].rearrange("a p t -> p a t"), writes=[cs[bi][1]])
                    for (o, w) in blks:
                        for (Wt, dstG, isf) in ((WI, GI, False), (WF, GF, True)):
                            bk, bb = S.bank()
                            for k in range(16):
                                S.mm(ps[0:40, bk, 0:w], Wt[:, k, :], hT[:, k, o:o + w], k == 0, k == 15, [bW, bhT], [bb], sig=(k == 15))
                            if not isf:
                                S.act(dstG[0:40, t0 + o:t0 + o + w], ps[0:40, bk, 0:w], AF.Identity, [bb, bC], [bG],
                                      bias=vec[0:40, V_GBI + l:V_GBI + l + 1])
                            else:
                                S.act(gtmp[:, :w], ps[0:40, bk, 0:w], AF.Exp, [bb, bC], [bgt], bias=smallv[0:40, 2:3], scale=-1.0)
                                S.act(gtmp[:, :w], gtmp[:, :w], AF.Ln, [bgt], [bgt], bias=smallv[0:40, 3:4])
                                S.ts("dve", dstG[0:40, t0 + o:t0 + o + w], gtmp[:, :w], -1.0, None, ALU.mult, None, [bgt], [bG])
                    for (c0, ncn) in PIECES:
                        wt, wb = wst.get(wi); wi += 1
                        for cc in range(ncn):
                            ch = c0 + cc
                            kind, dst, row = chunk_info(ch)
                            st_t, st_b = stg[sk % 3]; sk += 1
                            bkl = [S.bank() for _ in blks]
                            for k in range(16):
                                for bi, (o, w) in enumerate(blks):
                                    S.mm(PS(bkl[bi][0], w), wt[:, k, cc * 128:(cc + 1) * 128], hT[:, k, o:o + w], k == 0, k == 15,
                                         [wb, bhT], [bkl[bi][1]], sig=(k == 15 and bi == len(blks) - 1))
                            for bi, (o, w) in enumerate(blks):
                                bk, bb = bkl[bi]
                                if kind == "scale":
                                    S.act(st_t[:, o:o + w], PS(bk, w), AF.Identity, [bb], [st_b], scale=SC)
                                elif kind == "copy":
                                    if bi % 2 == 0:
                                        S.cp("dve", st_t[:, o:o + w], PS(bk, w), [bb], [st_b])
                                    else:
                                        S.cp("act", st_t[:, o:o + w], PS(bk, w), [bb], [st_b])
                                elif kind == "sig":
                                    S.act(st_t[:, o:o + w], PS(bk, w), AF.Sigmoid, [bb], [st_b])
                                else:
                                    gcol = smallv[:, 1:2] if kind == "q" else vec[:, V_KN + l:V_KN + l + 1]
                                    S.act(sqb[:, :w], PS(bk, w), AF.Square, [bb], [bsq])
                                    b2, bb2 = S.bank()
                                    S.mm(PS(b2, w), av2_b, sqb[:, :w], True, True, [bsq, bC], [bb2])
                                    S.act(f32t[:, 0, :w], PS(b2, w), AF.Sqrt, [bb2, bC], [bft[0]], bias=smallv[:, 0:1])
                                    S.op("dve", lambda g, w=w: g.reciprocal(f32t[:, 0, :w], f32t[:, 0, :w]), [bft[0]], [bft[0]])
                                    S.stt(f32t[:, 1, :w], PS(bk, w), gcol, f32t[:, 0, :w], ALU.mult, ALU.mult, [bb, bft[0], bC], [bft[1]])
                                    if s == 0:
                                        b3, bb3 = S.bank()
                                        S.mm(PS(b3, w), cf[:, CF_ROT:CF_ROT + 128], f32t[:, 1, :w], True, True, [bft[1], bC], [bb3])
                                        S.tt("dve", f32t[:, 2, :w], f32t[:, 1, :w], cs[bi][0][:, 0, :w], ALU.mult, [bft[1], cs[bi][1]], [bft[2]])
                                        S.tt("dve", f32t[:, 3, :w], PS(b3, w), cs[bi][0][:, 1, :w], ALU.mult, [bb3, cs[bi][1]], [bft[3]])
                                        S.tt("pool", st_t[:, o:o + w], f32t[:, 2, :w], f32t[:, 3, :w], ALU.add, [bft[2], bft[3]], [st_b])
                                    else:
                                        S.cp("act", st_t[:, o:o + w], f32t[:, 1, :w], [bft[1]], [st_b])
                            S.dma("sp", dst[row:row + 128, t0:t0 + n], st_t[:, :n], reads=[st_b], writes=[bPF if dst is PF else bQKR])
                S.barrier()

        def vproj_stage(l):
            wv = w_in[l].rearrange("(k p) n -> p k n", p=128)
            with ExitStack() as es:
                hTs = [(sb(es, "vhT%d" % i, [128, 16, 1024], BF16), Buf()) for i in range(2)]
                WV = sb(es, "WV", [128, 16, 1792], BF16); bWV = Buf()
                stg = [(sb(es, "vst%d" % i, [128, 1792], BF16), Buf()) for i in range(2)]
                for (c0, n_, o0) in ((1024, 512, 0), (2304, 256, 512), (4608, 512, 768), (5120, 512, 1280)):
                    S.dma("pool", WV[:, :, o0:o0 + n_], wv[:, :, c0:c0 + n_], writes=[bWV])
                ht_loads = []
                for (t0, n, s) in TILES:
                    def ldh(slot, t0=t0, n=n):
                        for c4 in range(4):
                            S.dma("sp", slot[0][:, c4 * 4:c4 * 4 + 4, :n], HTv[:, c4 * 4:c4 * 4 + 4, t0:t0 + n], reads=[bHT], writes=[slot[1]])
                    ht_loads.append(ldh)
                hst = Stream(hTs, ht_loads)
                sk = 0
                cbl = [(0, 512), (512, 512), (1024, 512), (1536, 256)]
                for ti, (t0, n, s) in enumerate(TILES):
                    hT, bhT = hst.get(ti)
                    for tt_ in range(n // 128):
                        bkl = [S.bank() for _ in cbl]
                        for k in range(16):
                            for ci, (c0, w) in enumerate(cbl):
                                S.mm(PS(bkl[ci][0], w), hT[:, k, tt_ * 128:(tt_ + 1) * 128], WV[:, k, c0:c0 + w], k == 0, k == 15,
                                     [bWV, bhT], [bkl[ci][1]], sig=(k == 15 and ci == 3))
                        st_t, st_b = stg[sk % 2]; sk += 1
                        for ci, (c0, w) in enumerate(cbl):
                            S.cp("dve" if ci % 2 == 0 else "act", st_t[:, c0:c0 + w], PS(bkl[ci][0], w), [bkl[ci][1]], [st_b])
                        S.dma("sp", VT[t0 + tt_ * 128:t0 + (tt_ + 1) * 128, :], st_t[:], reads=[st_b], writes=[bVT])
                S.barrier()

        VTv = VT.rearrange("(t p) d -> p t d", p=128)

        def gqa_stage(l, need_ctx, own=False):
            with ExitStack() as es:
                kT = sb(es, "gkT", [128, NTOK], BF16); bk_ = Buf()
                Vt = sb(es, "gV", [128, NCH, 128], BF16); bv_ = Buf()
                qT = sb(es, "gqT", [128, NTOK], BF16); bq_ = Buf()
                pT = [(sb(es, "gpT%d" % i, [128, 512], BF16), Buf()) for i in range(4)]
                rd = sb(es, "grd", [128, 512], F32); brd = Buf()
                ys = [(sb(es, "gys%d" % i, [128, 512], BF16), Buf()) for i in range(2)]
                qo = [(sb(es, "gqo%d" % i, [128, 512], BF16), Buf()) for i in range(2)]
                qk_ = 0
                yk = 0
                pk = 0
                for g in range(2):
                    S.dma("sp", kT[:], PF[R_GK + g * 128:R_GK + (g + 1) * 128, :], reads=[bPF], writes=[bk_])
                    S.dma("sp", Vt[:], VTv[:, :, 512 + g * 128:512 + (g + 1) * 128], reads=[bVT], writes=[bv_])
                    for qh in range(2):
                        h = 2 * g + qh
                        S.dma("sp", qT[:], PF[R_GQ + h * 128:R_GQ + (h + 1) * 128, :], reads=[bPF], writes=[bq_])
                        qblocks = [(i * 512, 512, list(range(NCH))) for i in range(4 if own else 8)]
                        if need_ctx:
                            qblocks.append((T, C, [32, 33]))
                        for (q0, w, kts) in qblocks:
                            if own:
                                qs_t, qs_b = qo[qk_ % 2]; qk_ += 1
                                S.act(qs_t[:, :w], qT[:, q0:q0 + w], AF.Identity, [bq_, bC], [qs_b], scale=vec[:, V_M:V_M + 1])
                                S.stt(qs_t[:, :w], qT[:, TL + q0:TL + q0 + w], vec[:, V_M + 1:V_M + 2], qs_t[:, :w], ALU.mult, ALU.add, [bq_, qs_b, bC], [qs_b])
                                q_ap, q_bf = qs_t[:, :w], qs_b
                            else:
                                q_ap, q_bf = qT[:, q0:q0 + w], bq_
                            bo, bbo = S.bankg('o', [0, 1])
                            bd, bbd = S.bankg('d', [2, 3])
                            pend = None
                            for i, kt in enumerate(kts + [None]):
                                cur = None
                                if kt is not None:
                                    bs, bbs = S.bankg('s', [4, 5, 6, 7])
                                    S.mm(PS(bs, w), kT[:, kt * 128:(kt + 1) * 128], q_ap, True, True, [bk_, q_bf], [bbs])
                                    p_t, p_b = pT[pk % 4]; pk += 1
                                    S.act(p_t[:, :w], PS(bs, w), AF.Exp, [bbs], [p_b])
                                    cur = (kt, p_t, p_b, i)
                                if pend is not None:
                                    kt_, p_t_, p_b_, i_ = pend
                                    S.mm(PS(bo, w), Vt[:, kt_, :], p_t_[:, :w], i_ == 0, i_ == len(kts) - 1, [bv_, p_b_], [bbo], sig=False)
                                    S.mm(PS(bd, w), ones_b, p_t_[:, :w], i_ == 0, i_ == len(kts) - 1, [bC, p_b_], [bbd])
                                pend = cur
                            S.op("dve", lambda e, w=w, bd=bd: e.reciprocal(rd[:, :w], PS(bd, w)), [bbd], [brd])
                            y_t, y_b = ys[yk % 2]; yk += 1
                            S.tt("dve", y_t[:, :w], PS(bo, w), rd[:, :w], ALU.mult, [bbo, brd], [y_b])
                            S.dma("sp", YT[512 + h * 128:512 + (h + 1) * 128, q0:q0 + w], y_t[:, :w], reads=[y_b], writes=[bYT])
                            if own:
                                S.dma("sp", YT[512 + h * 128:512 + (h + 1) * 128, TL + q0:TL + q0 + w], y_t[:, :w], reads=[y_b], writes=[bYT])
                S.barrier()

        def na_rows(need_ctx):
            rows = []
            for r in range(64):
                rs = min(max(r - 4, 0), 56)
                tl = []
                for j in range(rs // 2, (rs + 7) // 2 + 1):
                    v0 = rs <= 2 * j <= rs + 7
                    v1 = rs <= 2 * j + 1 <= rs + 7
                    d0 = 2 * j - r
                    if v0 and v1:
                        idx = d0 + 7
                        assert 0 <= idx < 14
                    elif v1:
                        assert d0 == -5
                        idx = 14
                    else:
                        assert v0 and d0 == 3
                        idx = 15
                    tl.append((j, idx))
                tl += [(32, None), (33, None)]
                rows.append((r * 64, tl))
            if need_ctx:
                for i in range(4):
                    rows.append((T + i * 64, [(32, None), (33, None)]))
            return rows

        def na_stage(l, need_ctx):
            rows = na_rows(need_ctx)
            with ExitStack() as es:
                kT = sb(es, "nkT", [128, NTOK], BF16); bk_ = Buf()
                Vt = sb(es, "nV", [128, NCH, 128], BF16); bv_ = Buf()
                qT = sb(es, "nqT", [128, NTOK], BF16); bq_ = Buf()
                BT = sb(es, "nBT", [128, 16, 64], F32); bbt = Buf()
                BTb = sb(es, "nBTb", [128, 16, 64], BF16); bbtb = Buf()
                pT = [(sb(es, "npT%d" % i, [128, 512], BF16), Buf()) for i in range(4)]
                rd = sb(es, "nrd", [128, 512], F32); brd = Buf()
                ys = [(sb(es, "nys%d" % i, [128, 512], BF16), Buf()) for i in range(2)]
                yk = 0
                pk = 0
                for h in range(4):
                    S.dma("sp", kT[:], PF[R_NAK + h * 128:R_NAK + (h + 1) * 128, :], reads=[bPF], writes=[bk_])
                    S.dma("sp", qT[:], PF[R_NAQ + h * 128:R_NAQ + (h + 1) * 128, :], reads=[bPF], writes=[bq_])
                    S.dma("sp", Vt[:], VTv[:, :, h * 128:(h + 1) * 128], reads=[bVT], writes=[bv_])
                    S.dma("sp", BT[:], nabt_d[l, h], writes=[bbt])
                    S.cp("dve", BTb[:], BT[:], [bbt], [bbtb])
                    for g0 in range(0, len(rows), 8):
                        grp = rows[g0:g0 + 8]
                        bo, bbo = S.bankg('o', [0, 1])
                        bd, bbd = S.bankg('d', [2, 3])
                        pend = None
                        for si, rw in enumerate(grp + [None]):
                            cur = None
                            if rw is not None:
                                q0, tl = rw
                                bs, bbs = S.bankg('s', [4, 5, 6, 7])
                                for i, (j, idx) in enumerate(tl):
                                    lastt = (i == len(tl) - 1)
                                    S.mm(PS(bs, 64, i * 64), kT[:, j * 128:(j + 1) * 128], qT[:, q0:q0 + 64], True, idx is None,
                                         [bk_, bq_], [bbs], sig=(lastt and idx is None))
                                    if idx is not None:
                                        S.mm(PS(bs, 64, i * 64), ident_b, BTb[:, idx, :], False, True, [bC, bbtb], [bbs], sig=lastt)
                                p_t, p_b = pT[pk % 4]; pk += 1
                                nt = len(tl)
                                S.act(p_t[:, :nt * 64], PS(bs, nt * 64), AF.Exp, [bbs], [p_b])
                                cur = (tl, p_t, p_b, si)
                            if pend is not None:
                                tl_, p_t_, p_b_, si_ = pend
                                for i, (j, idx) in enumerate(tl_):
                                    S.mm(PS(bo, 64, si_ * 64), Vt[:, j, :], p_t_[:, i * 64:(i + 1) * 64], i == 0, i == len(tl_) - 1,
                                         [bv_, p_b_], [bbo], sig=False)
                                for i, (j, idx) in enumerate(tl_):
                                    S.mm(PS(bd, 64, si_ * 64), ones_b, p_t_[:, i * 64:(i + 1) * 64], i == 0, i == len(tl_) - 1,
                                         [bC, p_b_], [bbd], sig=(i == len(tl_) - 1))
                            pend = cur
                        w = len(grp) * 64
                        q00 = grp[0][0]
                        S.op("dve", lambda e, w=w, bd=bd: e.reciprocal(rd[:, :w], PS(bd, w)), [bbd], [brd])
                        y_t, y_b = ys[yk % 2]; yk += 1
                        S.tt("dve", y_t[:, :w], PS(bo, w), rd[:, :w], ALU.mult, [bbo, brd], [y_b])
                        S.dma("sp", YT[h * 128:(h + 1) * 128, q00:q00 + w], y_t[:, :w], reads=[y_b], writes=[bYT])
                S.barrier()

        def conv_stage(l):
            NP = NTOK + 6
            with ExitStack() as es:
                xin = [(sb(es, "cx%d" % i, [128, NP], BF16), Buf()) for i in range(2)]
                acc = [(sb(es, "ca%d" % i, [128, NP], F32), Buf()) for i in range(2)]
                yo = [(sb(es, "cy%d" % i, [128, NP], BF16), Buf()) for i in range(2)]
                for i in range(2):
                    S.op("pool", lambda g, i=i: g.memset(xin[i][0][:], 0.0), [], [xin[i][1]])
                for c in range(16):
                    x_t, x_b = xin[c % 2]; a_t, a_b = acc[c % 2]; y_t, y_b = yo[c % 2]
                    S.dma("sp", x_t[:, 1:1 + T], QKR[c * 128:(c + 1) * 128, 0:T], reads=[bQKR], writes=[x_b])
                    S.dma("sp", x_t[:, T + 3:T + 3 + C], QKR[c * 128:(c + 1) * 128, T:NTOK], reads=[bQKR], writes=[x_b])
                    wc = lambda k: vec[:, V_CW + l * 48 + c * 3 + k:V_CW + l * 48 + c * 3 + k + 1]
                    n_ = NP - 2
                    S.act(a_t[:, 0:n_], x_t[:, 0:n_], AF.Identity, [x_b, bC], [a_b], scale=wc(0))
                    S.stt(a_t[:, 0:n_], x_t[:, 1:1 + n_], wc(1), a_t[:, 0:n_], ALU.mult, ALU.add, [x_b, a_b, bC], [a_b])
                    S.stt(a_t[:, 0:n_], x_t[:, 2:2 + n_], wc(2), a_t[:, 0:n_], ALU.mult, ALU.add, [x_b, a_b, bC], [a_b])
                    S.act(y_t[:, 0:n_], a_t[:, 0:n_], AF.Silu, [a_b, bC], [y_b], bias=vec[:, V_CB + l * 16 + c:V_CB + l * 16 + c + 1])
                    if c >= 8:
                        S.act(y_t[:, 0:n_], y_t[:, 0:n_], AF.Identity, [y_b], [y_b], scale=SC)
                    row = R_MQ + c * 128
                    S.dma("sp", PF[row:row + 128, 0:T], y_t[:, 0:T], reads=[y_b], writes=[bPF])
                    S.dma("sp", PF[row:row + 128, T:NTOK], y_t[:, T + 2:T + 2 + C], reads=[y_b], writes=[bPF])
                S.barrier()

        def mlstm_stage(l, GI, GF, bG, need_ctx=True):
            cf_order = [32, 33] + list(range(32))
            cb_order = [33, 32] + list(range(31, -1, -1))
            sel = lambda dh: cf[0:40, CF_SEL + dh * 128:CF_SEL + (dh + 1) * 128]
            with ExitStack() as es:
                NEGM = sb(es, "NEGM", [40, NCH, 128], F32)
                NEGm = sb(es, "NEGm", [40, NCH, 128], F32)
                Ucol = sb(es, "Ucol", [128, NCH, 40], F32)
                Wcol = sb(es, "Wcol", [128, NCH, 40], F32)
                CSb = sb(es, "CSb", [128, 16, 3 * NCH], F32)
                bR = Buf()
                G3 = GI[0:40, :].rearrange("p (c l) -> p c l", l=128)
                F3 = GF[0:40, :].rearrange("p (c l) -> p c l", l=128)
                with ExitStack() as pes:
                    tA = sb(pes, "mtA", [40, NCH, 128], F32)
                    tB = sb(pes, "mtB", [40, NCH, 128], F32)
                    tC = sb(pes, "mtC", [40, NCH, 128], F32)
                    sm = sb(pes, "msm", [40, 8, NCH], F32)
                    CHS = sb(pes, "mCHS", [40, 3, NCH], F32)
                    bP = bG

                    def dscan(x0, bufs, op):
                        X = x0
                        k = 0
                        for s_ in (1, 2, 4, 8, 16, 32, 64):
                            Y = bufs[k % len(bufs)]; k += 1
                            S.cp("act", Y[:], X[:] if X is not x0 else x0, [bP], [bP])
                            Xv = X[:] if X is not x0 else x0
                            S.tt("dve", Y[0:8, :, s_:], Xv[0:8, :, s_:], Xv[0:8, :, :128 - s_], op, [bP], [bP])
                            S.tt("dve", Y[32:40, :, :128 - s_], Xv[32:40, :, :128 - s_], Xv[32:40, :, s_:], op, [bP], [bP])
                            X = Y
                        return X

                    S.op("pool", lambda g: g.memset(NEGM[:], 0.0), [], [bP])
                    S.op("pool", lambda g: g.memset(NEGm[:], 0.0), [], [bP])
                    Bt = dscan(F3, [tA, tC], ALU.add)
                    assert Bt is tA
                    S.tt("dve", G3, G3, tA[:], ALU.subtract, [bP], [bP])
                    CMt = dscan(G3, [tB, tC], ALU.max)
                    assert CMt is tB
                    for i in range(NCH):
                        S.cp("pool", sm[0:8, 0, i:i + 1], tA[0:8, cf_order[i], 127:128], [bP], [bP])
                        S.cp("pool", sm[32:40, 0, i:i + 1], tA[32:40, cb_order[i], 0:1], [bP], [bP])
                        S.cp("pool", sm[0:8, 1, i:i + 1], tB[0:8, cf_order[i], 127:128], [bP], [bP])
                        S.cp("pool", sm[32:40, 1, i:i + 1], tB[32:40, cb_order[i], 0:1], [bP], [bP])
                    S.op("dve", lambda g: g.memset(sm[:, 3, :], 0.0), [], [bP])
                    for i in range(NCH):
                        S.tt("dve", sm[:, 4, i:i + 1], sm[:, 3, i:i + 1], sm[:, 1, i:i + 1], ALU.max, [bP], [bP])
                        if i < NCH - 1:
                            S.tt("dve", sm[:, 3, i + 1:i + 2], sm[:, 4, i:i + 1], sm[:, 0, i:i + 1], ALU.add, [bP], [bP])
                    S.cp("dve", CHS[:, 0, :], sm[:, 3, :], [bP], [bP])
                    S.tt("dve", CHS[:, 1, :], sm[:, 3, :], sm[:, 4, :], ALU.subtract, [bP], [bP])
                    S.tt("dve", CHS[:, 2, :], sm[:, 1, :], sm[:, 4, :], ALU.subtract, [bP], [bP])
                    S.act(CHS[:, 1:3, :], CHS[:, 1:3, :], AF.Exp, [bP], [bP])
                    for i in range(NCH):
                        S.ts("dve", NEGM[0:8, cf_order[i], :], tB[0:8, cf_order[i], :], sm[0:8, 3, i:i + 1], None, ALU.max, None, [bP], [bP])
                        S.ts("dve", NEGM[32:40, cb_order[i], :], tB[32:40, cb_order[i], :], sm[32:40, 3, i:i + 1], None, ALU.max, None, [bP], [bP])
                    for (a, b_) in ((0, 8), (32, 40)):
                        S.tt("dve", NEGm[a:b_], tA[a:b_], NEGM[a:b_], ALU.add, [bP], [bP])
                        S.ts("dve", NEGm[a:b_], NEGm[a:b_], -1.0, None, ALU.mult, None, [bP], [bP])
                        S.ts("dve", NEGM[a:b_], NEGM[a:b_], -1.0, None, ALU.mult, None, [bP], [bP])
                    S.cp("pool", sm[0:8, 5, :], tB[0:8, :, 127], [bP], [bP])
                    S.cp("pool", sm[32:40, 5, :], tB[32:40, :, 0], [bP], [bP])
                    for (a, b_) in ((0, 8), (32, 40)):
                        S.tt("dve", tA[a:b_], G3[a:b_], sm[a:b_, 5, :].unsqueeze(2).to_broadcast([b_ - a, NCH, 128]), ALU.subtract, [bP], [bP])
                    S.act(tA[:], tA[:], AF.Exp, [bP], [bP])
                    for (src, dstc) in ((G3, Ucol), (tA[:], Wcol)):
                        for c0 in range(0, NCH, 12):
                            ncn = min(12, NCH - c0)
                            bk, bb = S.bank()
                            for cc in range(ncn):
                                S.mm(PS(bk, 40, cc * 40), src[:, c0 + cc, :], cf[0:40, CF_ID:CF_ID + 40], True, True, [bP, bC], [bb], sig=(cc == ncn - 1))
                            S.cp("dve", dstc[:, c0:c0 + ncn, :], PS(bk, ncn * 40).rearrange("p (c j) -> p c j", j=40), [bb], [bR])
                    for d0 in range(0, 16, 4):
                        bk, bb = S.bank()
                        for dd in range(4):
                            S.mm(PS(bk, 3 * NCH, dd * 3 * NCH), sel(d0 + dd), CHS[:].rearrange("p a c -> p (a c)"), True, True, [bP, bC], [bb], sig=(dd == 3))
                        S.cp("dve", CSb[:, d0:d0 + 4, :], PS(bk, 4 * 3 * NCH).rearrange("p (d n) -> p d n", n=3 * NCH), [bb], [bR])
                    S.cp("pool", tC[:], NEGM[:], [bP], [bR])
                    S.barrier()
                qT = sb(es, "mqT", [128, NTOK], BF16); bq_ = Buf()
                kT = sb(es, "mkT", [128, NTOK], BF16); bk_ = Buf()
                Vt = sb(es, "mV", [128, NCH, 129], BF16); bv_ = Buf()
                Kt = sb(es, "mKt", [128, NCH, 128], BF16); bkt = Buf()
                osg = GF[:].bitcast(BF16); bos = Buf()
                Hacc = GI; bH = Buf()
                Cst = [[sb(es, "mC%d_%d" % (d_, i), [128, 129], F32) for i in range(2)] for d_ in range(2)]
                bSt = [[Buf(), Buf()] for d_ in range(2)]
                CLs = [sb(es, "mCLs%d" % d_, [128, NCH, 129], F32) for d_ in range(2)]; bCL = [Buf(), Buf()]
                Call = [sb(es, "mCall%d" % d_, [128, NCH, 129], BF16) for d_ in range(2)]; bCa = [Buf(), Buf()]
                f32s = [[sb(es, "mf%d_%d" % (i, j), [128, 128], F32) for j in range(5)] for i in range(3)]
                b16s = [[sb(es, "mb%d_%d" % (i, j), [128, 128], BF16) for j in range(3)] for i in range(3)]
                bfs = [[Buf() for j in range(5)] for i in range(3)]
                bbs_ = [[Buf() for j in range(3)] for i in range(3)]
                sq = sb(es, "msq", [128, 512], BF16); bsq = Buf()
                rsn = sb(es, "mrs", [128, 2, 512], F32); brs = Buf()
                ys = [(sb(es, "mys%d" % i, [128, 512], BF16), Buf()) for i in range(2)]
                yk = 0
                it = 0
                for h in range(8):
                    S.dma("sp", qT[:], PF[R_MQ + h * 128:R_MQ + (h + 1) * 128, :], reads=[bPF], writes=[bq_])
                    S.dma("sp", kT[:], PF[R_MK + h * 128:R_MK + (h + 1) * 128, :], reads=[bPF], writes=[bk_])
                    S.dma("sp", osg[:, 0:NTOK], PF[R_MO + h * 128:R_MO + (h + 1) * 128, :], reads=[bPF], writes=[bos])
                    S.dma("sp", Vt[:, :, 0:128], VTv[:, :, 768 + h * 128:768 + (h + 1) * 128], reads=[bVT], writes=[bv_])
                    S.op("pool", lambda g: g.memset(Vt[:, :, 128:129], 1.0), [], [bv_])
                    for c0 in range(0, NCH, 8):
                        ncn = min(8, NCH - c0)
                        bk, bb = S.bankg("mD", [6, 7])
                        pst = ps[:, bk, :].bitcast(BF16)
                        for cc in range(ncn):
                            S.op("pe", lambda e, cc=cc, c0=c0, pst=pst: e.transpose(pst[:, cc * 128:(cc + 1) * 128], kT[:, (c0 + cc) * 128:(c0 + cc + 1) * 128], ident_b),
                                 [bk_, bC], [bb], sig=(cc == ncn - 1))
                        S.cp("act", Kt[:, c0:c0 + ncn, :], pst[:, 0:ncn * 128].rearrange("p (c d) -> p c d", d=128), [bb], [bkt])
                    for dr in range(2):
                        dh = dr * 8 + h
                        row = h + 32 * dr
                        order = cf_order if dr == 0 else cb_order
                        pos = {c: i for i, c in enumerate(order)}
                        msk = cf[:, CF_MSK + dr * 128:CF_MSK + (dr + 1) * 128]
                        for i, c in enumerate(order):
                            st_ = it % 3; it += 1
                            B16 = b16s[st_]; bB = bbs_[st_]
                            inp_ = CSb[:, dh, 2 * NCH + i:2 * NCH + i + 1]
                            S.act(B16[2][:], Kt[:, c, :], AF.Identity, [bkt, bR], [bB[2]], scale=Wcol[:, c, row:row + 1])
                            bCk, bbC = S.bankg("mC", [4, 5])
                            S.mm(PS(bCk, 129, 0), B16[2][:], Vt[:, c, :], True, True, [bB[2], bv_], [bbC])
                            S.ts("dve", CLs[dr][:, i, :], PS(bCk, 129, 0), inp_, None, ALU.mult, None, [bbC, bR], [bCL[dr]])
                    for dr in range(2):
                        S.op("pool", lambda g, dr=dr: g.memset(Cst[dr][0][:], 0.0), [], [bSt[dr][0]])
                        S.op("pool", lambda g, dr=dr: g.memset(Call[dr][:, 0, :], 0.0), [], [bCa[dr]])
                    for i in range(NCH - 1):
                        cur = i % 2; nxt = 1 - cur
                        for dr in range(2):
                            dec = CSb[:, dr * 8 + h, NCH + i:NCH + i + 1]
                            S.stt(Cst[dr][nxt][:], Cst[dr][cur][:], dec, CLs[dr][:, i, :], ALU.mult, ALU.add,
                                  [bSt[dr][cur], bCL[dr], bR], [bSt[dr][nxt]])
                            S.cp("act", Call[dr][:, i + 1, :], Cst[dr][nxt][:], [bSt[dr][nxt]], [bCa[dr]])
                    for dr in range(2):
                        dh = dr * 8 + h
                        row = h + 32 * dr
                        order = cf_order if dr == 0 else cb_order
                        pos = {c: i for i, c in enumerate(order)}
                        msk = cf[:, CF_MSK + dr * 128:CF_MSK + (dr + 1) * 128]
                        groups = [(g0, 4) for g0 in range(0, 32, 4)] + ([(32, 2)] if need_ctx else [])
                        clist = []
                        for (g0, gn) in groups:
                            for j in range(gn):
                                clist.append((g0, gn, j))
                        gb = {}

                        def front(g0, gn, j, st_):
                            F = f32s[st_]; B16 = b16s[st_]; bF = bfs[st_]; bB = bbs_[st_]
                            c = g0 + j
                            i = pos[c]
                            tok = c * 128
                            if j == 0:
                                b1, bb1 = S.bankg("mR1", [0, 1])
                                b2, bb2 = S.bankg("mR2", [2, 3])
                                b3, bb3 = S.bankg("mS", [4, 5])
                                S.mm(PS(b1, gn * 128), sel(dh), NEGM[:, g0:g0 + gn, :], True, True, [bR, bC], [bb1])
                                S.mm(PS(b2, gn * 128), sel(dh), NEGm[:, g0:g0 + gn, :], True, True, [bR, bC], [bb2])
                                for jj in range(gn):
                                    t2 = (g0 + jj) * 128
                                    S.mm(PS(b3, 128, jj * 128), kT[:, t2:t2 + 128], qT[:, t2:t2 + 128], True, True, [bk_, bq_], [bb3], sig=(jj == gn - 1))
                                gb[g0] = (b1, bb1, b2, bb2, b3, bb3)
                            b1, bb1, b2, bb2, b3, bb3 = gb[g0]
                            S.act(F[0][:], PS(b1, 128, j * 128), AF.Exp, [bb1, bR], [bF[0]], bias=Ucol[:, c, row:row + 1])
                            S.act(F[1][:], PS(b1, 128, j * 128), AF.Exp, [bb1, bR], [bF[1]], bias=CSb[:, dh, i:i + 1])
                            S.act(F[2][:], PS(b2, 128, j * 128), AF.Exp, [bb2], [bF[2]])
                            S.tt("pool", F[0][:], F[0][:], msk, ALU.mult, [bF[0], bC], [bF[0]])
                            S.tt("dve", B16[0][:], PS(b3, 128, j * 128), F[0][:], ALU.mult, [bb3, bF[0]], [bB[0]])
                            S.tt("pool", B16[1][:], qT[:, tok:tok + 128], F[1][:], ALU.mult, [bq_, bF[1]], [bB[1]])
                            S.cp("pool", B16[2][:], Call[dr][:, i, 128:129].to_broadcast([128, 128]), [bCa[dr]], [bB[2]])

                        def back(g0, gn, j, st_):
                            F = f32s[st_]; B16 = b16s[st_]; bF = bfs[st_]; bB = bbs_[st_]
                            c = g0 + j
                            i = pos[c]
                            tok = c * 128
                            bBk, bbB = S.bankg("mB", [6, 7])
                            S.mm(PS(bBk, 128, 0), Vt[:, c, 0:128], B16[0][:], True, False, [bv_, bB[0]], [bbB], sig=False)
                            S.mm(PS(bBk, 128, 0), Call[dr][:, i, 0:128], B16[1][:], False, True, [bCa[dr], bB[1]], [bbB], sig=False)
                            S.mm(PS(bBk, 128, 128), ones_b, B16[0][:], True, False, [bC, bB[0]], [bbB], sig=False)
                            S.mm(PS(bBk, 128, 128), B16[2][:], B16[1][:], False, True, [bB[2], bB[1]], [bbB])
                            S.act(F[3][:], PS(bBk, 128, 128), AF.Abs, [bbB], [bF[3]])
                            S.tt("dve", F[3][:], F[3][:], F[2][:], ALU.max, [bF[3], bF[2]], [bF[3]])
                            S.op("dve", lambda g, F=F: g.reciprocal(F[3][:], F[3][:]), [bF[3]], [bF[3]])
                            if dr == 0:
                                S.tt("dve", Hacc[:, tok:tok + 128], PS(bBk, 128, 0), F[3][:], ALU.mult, [bbB, bF[3]], [bH])
                            else:
                                S.tt("dve", F[4][:], PS(bBk, 128, 0), F[3][:], ALU.mult, [bbB, bF[3]], [bF[4]])
                                S.tt("pool", Hacc[:, tok:tok + 128], Hacc[:, tok:tok + 128], F[4][:], ALU.add, [bH, bF[4]], [bH])

                        sets = []
                        for n_, (g0, gn, j) in enumerate(clist):
                            st_ = it % 3; it += 1
                            sets.append(st_)
                            front(g0, gn, j, st_)
                            if n_ > 0:
                                back(*clist[n_ - 1], sets[n_ - 1])
                        back(*clist[-1], sets[-1])
                    for (o, w) in blocks(NTOK if need_ctx else T):
                        S.act(sq[:, :w], Hacc[:, o:o + w], AF.Square, [bH], [bsq])
                        bk, bb = S.bankg("mD", [6, 7])
                        S.mm(PS(bk, w), av2_b, sq[:, :w], True, True, [bsq, bC], [bb])
                        S.act(rsn[:, 0, :w], PS(bk, w), AF.Sqrt, [bb, bC], [brs], bias=smallv[:, 0:1])
                        S.op("dve", lambda g, w=w: g.reciprocal(rsn[:, 0, :w], rsn[:, 0, :w]), [brs], [brs])
                        S.stt(rsn[:, 1, :w], Hacc[:, o:o + w], vec[:, V_ON + l * 8 + h:V_ON + l * 8 + h + 1], rsn[:, 0, :w], ALU.mult, ALU.mult, [bH, brs, bC], [brs])
                        y_t, y_b = ys[yk % 2]; yk += 1
                        S.tt("dve", y_t[:, :w], rsn[:, 1, :w], osg[:, o:o + w], ALU.mult, [brs, bos], [y_b])
                        S.dma("sp", YT[1024 + h * 128:1024 + (h + 1) * 128, o:o + w], y_t[:, :w], reads=[y_b], writes=[bYT])
                S.barrier()

        def wout_stage(l, tiles, split=False):
            wo = w_out[l].rearrange("(c p) n -> p c n", p=128)
            YTv = YT.rearrange("(c p) t -> p c t", p=128)
            m0 = vec[:, V_M:V_M + 1]
            m1 = vec[:, V_M + 1:V_M + 2]
            with ExitStack() as es:
                xs = sb(es, "oxs", [128, 16, 1024], F32); bxs = Buf()
                yT = sb(es, "oyT", [128, 16, 1024], BF16); byT = Buf()
                wos = [(sb(es, "owo%d" % i, [128, 16, 512], BF16), Buf()) for i in range(2)]
                if split:
                    xB = [(sb(es, "oxB%d" % i, [128, 4, 1024], F32), Buf()) for i in range(2)]
                    yB = [(sb(es, "oyB%d" % i, [128, 4, 1024], BF16), Buf()) for i in range(2)]
                loads = []
                for _ in tiles:
                    for dp in range(4):
                        loads.append(lambda slot, dp=dp: S.dma("pool", slot[0][:], wo[:, :, dp * 512:(dp + 1) * 512], writes=[slot[1]]))
                wst = Stream(wos, loads)
                wi = 0
                bk_ = 0
                for (t0, n, s) in tiles:
                    blks = blocks(n)
                    for c4 in range(4):
                        cs_ = slice(c4 * 4, c4 * 4 + 4)
                        S.dma("sp", xs[:, cs_, :n], XTv[:, cs_, t0:t0 + n], reads=[bXT], writes=[bxs])
                        S.dma("sp", yT[:, cs_, :n], YTv[:, cs_, t0:t0 + n], reads=[bYT], writes=[byT])
                        if split:
                            xb_t, xb_b = xB[bk_ % 2]; yb_t, yb_b = yB[bk_ % 2]; bk_ += 1
                            S.dma("sp", xb_t[:, :, :n], XTv[:, cs_, TL + t0:TL + t0 + n], reads=[bXT], writes=[xb_b])
                            S.dma("sp", yb_t[:, :, :n], YTv[:, cs_, TL + t0:TL + t0 + n], reads=[bYT], writes=[yb_b])
                            S.act(xs[:, cs_, :n], xs[:, cs_, :n], AF.Identity, [bxs, bC], [bxs], scale=m0)
                            S.stt(xs[:, cs_, :n], xb_t[:, :, :n], m1, xs[:, cs_, :n], ALU.mult, ALU.add, [xb_b, bxs, bC], [bxs])
                            S.act(yT[:, cs_, :n], yT[:, cs_, :n], AF.Identity, [byT, bC], [byT], scale=m0)
                            S.stt(yT[:, cs_, :n], yb_t[:, :, :n], m1, yT[:, cs_, :n], ALU.mult, ALU.add, [yb_b, byT, bC], [byT])
                    for dp in range(4):
                        wt, wb = wst.get(wi); wi += 1
                        for dc in range(4):
                            d = dp * 4 + dc
                            bkl = [S.bank() for _ in blks]
                            for c in range(16):
                                for bi, (o, w) in enumerate(blks):
                                    S.mm(PS(bkl[bi][0], w), wt[:, c, dc * 128:(dc + 1) * 128], yT[:, c, o:o + w], c == 0, c == 15,
                                         [wb, byT], [bkl[bi][1]], sig=(c == 15 and bi == len(blks) - 1))
                            for bi, (o, w) in enumerate(blks):
                                S.stt(xs[:, d, o:o + w], PS(bkl[bi][0], w), AB[:, l, 1, 2, d, s:s + 1], xs[:, d, o:o + w],
                                      ALU.mult, ALU.add, [bkl[bi][1], bxs, bC], [bxs])
                    for c4 in range(4):
                        cs_ = slice(c4 * 4, c4 * 4 + 4)
                        if split:
                            S.dma("sp", XOv[:, cs_, t0:t0 + n], xs[:, cs_, :n], reads=[bxs], writes=[bXO])
                        else:
                            S.dma("sp", XTv[:, cs_, t0:t0 + n], xs[:, cs_, :n], reads=[bxs], writes=[bXT])
                S.barrier()

        for l in range(nlayers):
            last = (l == 1)
            ffn_stage(l, 0)
            if stop == "ffn1_%d" % l:
                dump_stage(XT, D)
                return nc
            mixnorm_stage(l)
            with ExitStack() as mes:
                GI = sb(mes, "GI", [128, NTOK], F32)
                GF = sb(mes, "GF", [128, NTOK], F32)
                bG = Buf()
                proj_stage(l, GI, GF, bG)
                vproj_stage(l)
                if stop == "proj_%d" % l:
                    dump_bf16(PF, 2048)
                    return nc
                gqa_stage(l, not last, own=last)
                if stop == "gqa_%d" % l:
                    dump_bf16(YT, 2048)
                    return nc
                na_stage(l, not last)
                if stop == "na_%d" % l:
                    dump_bf16(YT, 2048)
                    return nc
                conv_stage(l)
                if stop == "conv_%d" % l:
                    dump_bf16(PF[R_MQ:R_MQ + 2048], 2048)
                    return nc
                mlstm_stage(l, GI, GF, bG, need_ctx=not last)
                if stop == "ml_%d" % l:
                    dump_bf16(YT, 2048)
                    return nc
                if stop == "mlc_%d" % l:
                    dump_bf16(YT, 128, tok0=NTOK - T)
                    return nc
            if last:
                wout_stage(l, OWN, split=True)
            else:
                wout_stage(l, TILES)
            if stop == "mix_%d" % l:
                dump_stage(XT, D)
                return nc
            if last:
                ffn_stage(l, 1, OWN, xv=XOv, bxv=bXO)
            else:
                ffn_stage(l, 1)
        final_stage()
    return nc


def _consts():
    cf = np.zeros((128, NCF), np.float32)
    cf[:, CF_ID:CF_ID + 128] = np.eye(128, dtype=np.float32)
    R = np.zeros((128, 128), np.float32)
    for j in range(128):
        h = (j // 64) * 64
        jj = j % 64
        if jj < 32:
            R[h + jj + 32, j] = -1.0
        else:
            R[h + jj - 32, j] = 1.0
    cf[:, CF_ROT:CF_ROT + 128] = R
    s = np.arange(128)[:, None]; q = np.arange(128)[None, :]
    cf[:, CF_MSK:CF_MSK + 128] = (s <= q).astype(np.float32)
    cf[:, CF_MSK + 128:CF_MSK + 256] = (s >= q).astype(np.float32)
    for dh in range(16):
        row = (dh % 8) + (32 if dh >= 8 else 0)
        cf[row, CF_SEL + dh * 128:CF_SEL + (dh + 1) * 128] = 1.0
    cb = np.zeros((128, NCB), np.float32)
    cb[:, CB_ONE:CB_ONE + 128] = 1.0
    cb[:, CB_AVG:CB_AVG + 128] = 1.0 / 2048
    cb[:, CB_AV2:CB_AV2 + 128] = 1.0 / 128
    cb[:, CB_ID:CB_ID + 128] = np.eye(128)
    t = np.arange(T)
    row = (t // 64).astype(np.float32); col = (t % 64).astype(np.float32)
    half = 64
    inv = (1.0 / (np.float32(10000.0) ** (np.arange(0, half, 2, dtype=np.float32) / np.float32(half)))).astype(np.float32)
    ar = row[:, None] * inv[None, :]; ac = col[:, None] * inv[None, :]
    ang = np.concatenate([ar, ar, ac, ac], -1)
    rope = np.stack([np.cos(ang).T, np.sin(ang).T]).astype(np.float32)
    return cf, cb.astype(ml_dtypes.bfloat16), np.ascontiguousarray(rope)


def _nabt(rpb):
    L = rpb.shape[0]
    out = np.full((L, 4, 128, 16, 64), NEG, np.float32)
    qc = np.arange(64)
    cs = np.clip(qc - 8, 0, 48)
    kc = np.arange(64)
    col_in = (kc[None, :] >= cs[:, None]) & (kc[None, :] < cs[:, None] + 16)
    dc = np.clip(kc[None, :] - qc[:, None], -15, 15) + 15
    for idx in range(16):
        if idx < 14:
            d0, valid = idx - 7, (True, True)
        elif idx == 14:
            d0, valid = -5, (False, True)
        else:
            d0, valid = 3, (True, False)
        for rr in range(2):
            if not valid[rr]:
                continue
            dr = d0 + rr + 7
            if dr < 0 or dr > 14:
                continue
            vals = rpb[:, :, dr, :][:, :, dc]
            vals = np.where(col_in[None, None], vals, np.float32(NEG))
            out[:, :, rr * 64:(rr + 1) * 64, idx, :] = vals.transpose(0, 1, 3, 2)
    return out


def _vec(b, r, inp):
    v = np.zeros((128, NV), np.float32)
    fm = lambda a: np.asarray(a, np.float32).reshape(-1, 128).T
    cc = np.stack([fm(inp["c"][b]), fm(inp["c_ctx"])], -1)
    v[:, V_CC:V_CC + 32] = cc.reshape(128, 32)
    for l in range(2):
        v[:, V_BADA + l * 144:V_BADA + (l + 1) * 144] = fm(inp["b_ada"][l])
        for si, nm in enumerate(("norm_ff1", "norm_mix", "norm_ff2")):
            v[:, V_GAIN + (l * 3 + si) * 16:V_GAIN + (l * 3 + si) * 16 + 16] = fm(inp[nm][l])
        cw = np.stack([fm(inp["ml_conv_w"][l][k]) for k in range(3)], -1)
        v[:, V_CW + l * 48:V_CW + (l + 1) * 48] = cw.reshape(128, 48)
        v[:, V_CB + l * 16:V_CB + (l + 1) * 16] = fm(inp["ml_conv_b"][l])
        v[:, V_QN + l] = inp["gqa_q_norm"][l]
        v[:, V_KN + l] = inp["gqa_k_norm"][l]
        v[:, V_ON + l * 8:V_ON + (l + 1) * 8] = fm(inp["ml_out_norm"][l])
        gb = np.asarray(inp["ml_gate_b"][l], np.float32).reshape(4, 8)
        v[0:8, V_GBI + l] = gb[0]; v[32:40, V_GBI + l] = gb[2]
        v[0:8, V_GBF + l] = gb[1]; v[32:40, V_GBF + l] = gb[3]
    v[:, V_FIN:V_FIN + 16] = fm(inp["final_norm"])
    v[:, V_M + r] = 1.0
    return v


def _run(inp, stop=None, nlayers=2, ncores=NCORES):
    inp = {k: np.asarray(v) for k, v in inp.items()}
    cf, cb, rope = _consts()
    nabt = _nabt(np.asarray(inp["na_rpb"], np.float32))
    nc = build(stop=stop, nlayers=nlayers)
    shared = {"cf": cf, "cb": cb, "rope": rope, "nabt": nabt}
    for nm in ("w_ada", "ff1_gate", "ff1_up", "ff1_down", "ff2_gate", "ff2_up", "ff2_down", "w_in", "w_out"):
        shared[nm] = np.ascontiguousarray(inp[nm], dtype=np.float32)
    in_maps = []
    xts = {}
    for c in range(ncores):
        b, r = c // 2, c % 2
        m = dict(shared)
        if b not in xts:
            xts[b] = np.ascontiguousarray(np.concatenate([inp["x"][b].T, inp["ctx"][b].T], axis=1), dtype=np.float32)
        m["xT0"] = xts[b]
        m["vec"] = _vec(b, r, inp)
        in_maps.append(m)
    res = run_bass_kernel_spmd(nc, in_maps, core_ids=list(range(ncores)))
    return [np.asarray(r["outT"]) for r in res.results]


def kernel(**inputs):
    outs = _run(inputs)
    return np.stack([np.concatenate([outs[2 * b].T, outs[2 * b + 1].T], axis=0) for b in range(len(outs) // 2)]).astype(np.float32)
```
